# Optimizing a Trainium2 kernel written in Bass

```python
import jax, jax.numpy as jnp
from jax import lax
import numpy as np

D_MODEL = 1024
BATCH = 8
SEQ = 2048
DEPTH = 1
DEC_BATCH = 128
DEC_SEQ = 1
PAST_LEN = 16384
PAGE_SIZE = 128

SSM_EXPAND = 2
D_INNER = SSM_EXPAND * D_MODEL
SSM_HEADDIM = 64
SSM_HEADS = D_INNER // SSM_HEADDIM
SSM_GROUPS = 8
SSM_HPG = SSM_HEADS // SSM_GROUPS
SSM_STATE = 128
CONV_WIDTH = 4
CONV_DIM = D_INNER + 2 * SSM_GROUPS * SSM_STATE
SSD_CHUNK = 128
RWKV_DIM = D_MODEL
RWKV_HEADSIZE = 64
RWKV_HEADS = RWKV_DIM // RWKV_HEADSIZE
DECAY_LORA = 64
AAA_LORA = 64
GATE_LORA = 128
SHIFT_DIM = 3 * RWKV_DIM + DECAY_LORA + AAA_LORA + GATE_LORA
LN_X_EPS = 64e-5
IN_DIM = D_INNER + CONV_DIM + SSM_HEADS + SHIFT_DIM + 2 * D_MODEL
PEER_HEADS = 8
N_KEYS = 128
N_EXPERTS = N_KEYS * N_KEYS
PEER_QDIM = 256
PEER_TOPK = 16
PEER_BLOCK = 128
PLE_DIM = 256
EPS = 1e-6

kernel_name = 'hybrid_ssd_rwkv7_peer_step'


def rmsnorm(x, g):
    xf = x.astype(jnp.float32)
    y = xf * lax.rsqrt(jnp.mean(xf * xf, axis=-1, keepdims=True) + EPS)
    return (y * g.astype(jnp.float32)).astype(x.dtype)


def grouped_rmsnorm(x, g, n_groups):
    shp = x.shape
    xf = x.astype(jnp.float32).reshape(shp[:-1] + (n_groups, shp[-1] // n_groups))
    y = xf * lax.rsqrt(jnp.mean(xf * xf, axis=-1, keepdims=True) + EPS)
    return (y.reshape(shp) * g.astype(jnp.float32)).astype(x.dtype)


def split_proj(u):
    sizes = (D_INNER, CONV_DIM, SSM_HEADS, SHIFT_DIM, D_MODEL, D_MODEL)
    return jnp.split(u, np.cumsum(sizes)[:-1].tolist(), axis=-1)


def causal_conv(xbc, conv_prev, w, b):
    l = xbc.shape[1]
    full = jnp.concatenate([conv_prev, xbc], axis=1)
    out = b + sum(full[:, k:k + l] * w[k] for k in range(CONV_WIDTH))
    return jax.nn.silu(out), full[:, -(CONV_WIDTH - 1):]


def ssd_scan(xh, dt, a, bm, cm, h0, chunk):
    b, l = xh.shape[:2]
    c = l // chunk
    rs = lambda t: t.reshape((b, c, chunk) + t.shape[2:])
    xh, dt, bm, cm = rs(xh), rs(dt), rs(bm), rs(cm)
    a_cum = jnp.cumsum(dt * a, axis=2)
    seg = a_cum[:, :, :, None] - a_cum[:, :, None, :]
    causal = jnp.tril(jnp.ones((chunk, chunk), bool))[:, :, None, None]
    lmat = jnp.exp(jnp.where(causal, seg, -jnp.inf))
    cb = jnp.einsum('bclgn,bcsgn->bclsg', cm, bm)
    m = cb[..., None] * lmat * dt[:, :, None]
    y_diag = jnp.einsum('bclsgr,bcsgrp->bclgrp', m, xh)
    decay_out = jnp.exp(a_cum[:, :, -1:] - a_cum)
    states = jnp.einsum('bcsgn,bcsgr,bcsgrp->bcgrpn', bm, decay_out * dt, xh)
    chunk_decay = jnp.exp(a_cum[:, :, -1])

    def step(h, inp):
        st, dec = inp
        return h * dec[..., None, None] + st, h

    h_last, h_prev = lax.scan(step, h0.astype(states.dtype),
                              (jnp.moveaxis(states, 1, 0), jnp.moveaxis(chunk_decay, 1, 0)))
    h_prev = jnp.moveaxis(h_prev, 0, 1)
    y_off = jnp.einsum('bclgn,bcgrpn,bclgr->bclgrp', cm, h_prev, jnp.exp(a_cum))
    y = (y_diag + y_off).reshape((b, l) + y_diag.shape[3:])
    return y, h_last


def mamba_branch(z, xbc_raw, dt_raw, conv_prev, ssm_prev, conv_w, conv_b, dt_bias, a_log,
                 d_skip, ssm_norm, w_out_ssm):
    b, l, _ = z.shape
    xbc, conv_new = causal_conv(xbc_raw, conv_prev, conv_w, conv_b)
    xs, bm, cm = jnp.split(xbc, [D_INNER, D_INNER + SSM_GROUPS * SSM_STATE], axis=-1)
    xh = xs.reshape(b, l, SSM_GROUPS, SSM_HPG, SSM_HEADDIM)
    bm = bm.reshape(b, l, SSM_GROUPS, SSM_STATE)
    cm = cm.reshape(b, l, SSM_GROUPS, SSM_STATE)
    dt = jax.nn.softplus(dt_raw + dt_bias).reshape(b, l, SSM_GROUPS, SSM_HPG)
    a = -jnp.exp(a_log).reshape(SSM_GROUPS, SSM_HPG)
    h0 = ssm_prev.reshape(b, SSM_GROUPS, SSM_HPG, SSM_HEADDIM, SSM_STATE)
    chunk = SSD_CHUNK if l % SSD_CHUNK == 0 else l
    y, h_last = ssd_scan(xh, dt, a, bm, cm, h0, chunk)
    y = (y + xh * d_skip.reshape(SSM_GROUPS, SSM_HPG, 1)).reshape(b, l, D_INNER)
    y = grouped_rmsnorm(y * jax.nn.silu(z), ssm_norm, SSM_GROUPS)
    return y @ w_out_ssm, conv_new, h_last.reshape(b, SSM_HEADS, SSM_HEADDIM, SSM_STATE)


def wkv_scan(r, decay, k, v, a_vec, b_vec, s0):
    def step(s, inp):
        rt, wt, kt, vt, at, bt = inp
        s = (s * wt[:, :, None, :]
             + jnp.einsum('bhvk,bhk->bhv', s, at)[..., None] * bt[:, :, None, :]
             + vt[..., None] * kt[:, :, None, :])
        return s, jnp.einsum('bhvk,bhk->bhv', s, rt)

    seqs = tuple(jnp.moveaxis(t, 1, 0) for t in (r, decay, k, v, a_vec, b_vec))
    s_last, ys = lax.scan(step, s0.astype(r.dtype), seqs)
    return jnp.moveaxis(ys, 0, 1), s_last


def rwkv_branch(proj, shift_prev, wkv_prev, shift_mu, decay_w0, decay_w2, aaa_a0, aaa_a2,
                gate_g2, k_k, k_a, r_k, lnx_g, lnx_b, w_out_rwkv):
    b, l, _ = proj.shape
    prev = jnp.concatenate([shift_prev[:, None, :], proj[:, :-1]], axis=1)
    xs = proj + (prev - proj) * shift_mu
    offs = np.cumsum((RWKV_DIM, RWKV_DIM, RWKV_DIM, DECAY_LORA, AAA_LORA)).tolist()
    r, k, v, xw, xa, xg = jnp.split(xs, offs, axis=-1)
    w = -jax.nn.softplus(-(decay_w0 + jnp.tanh(xw) @ decay_w2)) - 0.5
    a = jax.nn.sigmoid(aaa_a0 + xa @ aaa_a2)
    g = jax.nn.sigmoid(xg) @ gate_g2
    hs = lambda t: t.reshape(b, l, RWKV_HEADS, RWKV_HEADSIZE)
    r, w, k, v, a = hs(r), hs(w), hs(k), hs(v), hs(a)
    kkf = (k * k_k).astype(jnp.float32)
    kk = (kkf * lax.rsqrt(jnp.sum(kkf * kkf, axis=-1, keepdims=True) + 1e-12)).astype(k.dtype)
    k = k * (1 + (a - 1) * k_a)
    decay = jnp.exp(-jnp.exp(w))
    y, s_last = wkv_scan(r, decay, k, v, -kk, kk * a, wkv_prev)
    yf = y.astype(jnp.float32)
    mu = jnp.mean(yf, axis=-1, keepdims=True)
    var = jnp.mean(jnp.square(yf - mu), axis=-1, keepdims=True)
    yn = ((yf - mu) * lax.rsqrt(var + LN_X_EPS)).reshape(b, l, RWKV_DIM)
    y = (yn * lnx_g.astype(jnp.float32) + lnx_b.astype(jnp.float32)).astype(proj.dtype)
    bonus = (jnp.sum(r * k * r_k, axis=-1, keepdims=True) * v).reshape(b, l, RWKV_DIM)
    return ((y + bonus) * g) @ w_out_rwkv, s_last, proj[:, -1]


def peer(h, peer_wq, peer_k1, peer_k2, peer_u, peer_v):
    b, l, d = h.shape
    t = b * l
    nblk = -(-t // PEER_BLOCK)
    hp = jnp.pad(h.reshape(t, d), ((0, nblk * PEER_BLOCK - t), (0, 0))).reshape(nblk, PEER_BLOCK, d)

    def block(xb):
        q = (xb @ peer_wq).reshape(PEER_BLOCK, PEER_HEADS, 2, PEER_QDIM // 2)
        s1 = jnp.einsum('thd,hnd->thn', q[:, :, 0], peer_k1)
        s2 = jnp.einsum('thd,hnd->thn', q[:, :, 1], peer_k2)
        v1, i1 = lax.top_k(s1, PEER_TOPK)
        v2, i2 = lax.top_k(s2, PEER_TOPK)
        cand = (v1[..., :, None] + v2[..., None, :]).reshape(PEER_BLOCK, PEER_HEADS, -1)
        cidx = (i1[..., :, None] * N_KEYS + i2[..., None, :]).reshape(PEER_BLOCK, PEER_HEADS, -1)
        top, pos = lax.top_k(cand, PEER_TOPK)
        idx = jnp.take_along_axis(cidx, pos, axis=-1)
        gate = jax.nn.softmax(top.astype(jnp.float32), axis=-1).astype(xb.dtype)
        act = jax.nn.gelu(jnp.einsum('thkd,td->thk', peer_u[idx], xb), approximate=False)
        return jnp.einsum('thk,thkd->td', gate * act, peer_v[idx])

    out = lax.map(block, hp).reshape(-1, d)[:t]
    return out.reshape(b, l, d)


def layer(x, p, conv_prev, ssm_prev, wkv_prev, shift_prev, lp):
    h = rmsnorm(x, lp['norm_mix'])
    z, xbc_raw, dt_raw, rw_proj, gate_m, gate_r = split_proj(h @ lp['w_in'])
    y_m, conv_new, ssm_new = mamba_branch(z, xbc_raw, dt_raw, conv_prev, ssm_prev, lp['conv_w'],
                                          lp['conv_b'], lp['dt_bias'], lp['a_log'], lp['d_skip'],
                                          lp['ssm_norm'], lp['w_out_ssm'])
    y_r, wkv_new, shift_new = rwkv_branch(rw_proj, shift_prev, wkv_prev, lp['shift_mu'],
                                          lp['decay_w0'], lp['decay_w2'], lp['aaa_a0'],
                                          lp['aaa_a2'], lp['gate_g2'], lp['k_k'], lp['k_a'],
                                          lp['r_k'], lp['lnx_g'], lp['lnx_b'], lp['w_out_rwkv'])
    x = x + (jax.nn.sigmoid(gate_m) * y_m + jax.nn.sigmoid(gate_r) * y_r) @ lp['w_out']
    x = x + peer(rmsnorm(x, lp['norm_ffn']), lp['peer_wq'], lp['peer_k1'], lp['peer_k2'],
                 lp['peer_u'], lp['peer_v'])
    x = x + jax.nn.sigmoid(rmsnorm(x, lp['norm_ple']) @ lp['w_ple_gate']) * (p @ lp['w_ple_proj'])
    return x, ssm_new, conv_new, wkv_new, shift_new


def setup_inputs(seed: int = 0) -> dict:
    key = jax.random.key(seed)
    ks = list(jax.random.split(key, 48))

    def nrm(shape, scale):
        return jax.random.normal(ks.pop(), shape, jnp.float32) * scale

    def unif(shape, lo, hi):
        return jax.random.uniform(ks.pop(), shape, jnp.float32, lo, hi)

    L = DEPTH
    dt0 = jnp.exp(unif((L, SSM_HEADS), float(np.log(1e-3)), float(np.log(1e-1))))
    return {
        'x_prompt': nrm((BATCH, SEQ, D_MODEL), 1.0),
        'x_sample': nrm((DEC_BATCH, DEC_SEQ, D_MODEL), 1.0),
        'p_prompt': nrm((L, BATCH, SEQ, PLE_DIM), 1.0),
        'p_sample': nrm((L, DEC_BATCH, DEC_SEQ, PLE_DIM), 1.0),
        'state_ssm': nrm((L, DEC_BATCH, SSM_HEADS, SSM_HEADDIM, SSM_STATE), 0.1),
        'state_conv': nrm((L, DEC_BATCH, CONV_WIDTH - 1, CONV_DIM), 1.0),
        'state_wkv': nrm((L, DEC_BATCH, RWKV_HEADS, RWKV_HEADSIZE, RWKV_HEADSIZE), 0.1),
        'state_shift': nrm((L, DEC_BATCH, SHIFT_DIM), 1.0),
        'norm_mix': 1.0 + nrm((L, D_MODEL), 0.02),
        'w_in': nrm((L, D_MODEL, IN_DIM), D_MODEL ** -0.5),
        'conv_w': nrm((L, CONV_WIDTH, CONV_DIM), CONV_WIDTH ** -0.5),
        'conv_b': nrm((L, CONV_DIM), 0.02),
        'dt_bias': dt0 + jnp.log(-jnp.expm1(-dt0)),
        'a_log': jnp.log(unif((L, SSM_HEADS), 1.0, 16.0)),
        'd_skip': 1.0 + nrm((L, SSM_HEADS), 0.02),
        'ssm_norm': 1.0 + nrm((L, D_INNER), 0.02),
        'w_out_ssm': nrm((L, D_INNER, D_MODEL), D_INNER ** -0.5),
        'shift_mu': unif((L, SHIFT_DIM), 0.0, 1.0),
        'decay_w0': unif((L, RWKV_DIM), -6.0, -1.0),
        'decay_w2': nrm((L, DECAY_LORA, RWKV_DIM), 0.5 * DECAY_LORA ** -0.5),
        'aaa_a0': nrm((L, RWKV_DIM), 0.1),
        'aaa_a2': nrm((L, AAA_LORA, RWKV_DIM), AAA_LORA ** -0.5),
        'gate_g2': nrm((L, GATE_LORA, RWKV_DIM), GATE_LORA ** -0.5),
        'k_k': 0.85 + nrm((L, RWKV_HEADS, RWKV_HEADSIZE), 0.02),
        'k_a': 1.0 + nrm((L, RWKV_HEADS, RWKV_HEADSIZE), 0.02),
        'r_k': nrm((L, RWKV_HEADS, RWKV_HEADSIZE), 0.1),
        'lnx_g': 1.0 + nrm((L, RWKV_DIM), 0.02),
        'lnx_b': nrm((L, RWKV_DIM), 0.02),
        'w_out_rwkv': nrm((L, RWKV_DIM, D_MODEL), RWKV_DIM ** -0.5),
        'w_out': nrm((L, D_MODEL, D_MODEL), D_MODEL ** -0.5),
        'norm_ffn': 1.0 + nrm((L, D_MODEL), 0.02),
        'peer_wq': nrm((L, D_MODEL, PEER_HEADS * PEER_QDIM), D_MODEL ** -0.5),
        'peer_k1': nrm((L, PEER_HEADS, N_KEYS, PEER_QDIM // 2), (PEER_QDIM // 2) ** -0.5),
        'peer_k2': nrm((L, PEER_HEADS, N_KEYS, PEER_QDIM // 2), (PEER_QDIM // 2) ** -0.5),
        'peer_u': nrm((L, N_EXPERTS, D_MODEL), D_MODEL ** -0.5),
        'peer_v': nrm((L, N_EXPERTS, D_MODEL), 0.2),
        'norm_ple': 1.0 + nrm((L, D_MODEL), 0.02),
        'w_ple_gate': nrm((L, D_MODEL, D_MODEL), D_MODEL ** -0.5),
        'w_ple_proj': nrm((L, PLE_DIM, D_MODEL), PLE_DIM ** -0.5),
        'norm_final': 1.0 + nrm((D_MODEL,), 0.02),
    }


def reference(x_prompt, x_sample, p_prompt, p_sample, state_ssm, state_conv, state_wkv,
              state_shift, norm_mix, w_in, conv_w, conv_b, dt_bias, a_log, d_skip, ssm_norm,
              w_out_ssm, shift_mu, decay_w0, decay_w2, aaa_a0, aaa_a2, gate_g2, k_k, k_a, r_k,
              lnx_g, lnx_b, w_out_rwkv, w_out, norm_ffn, peer_wq, peer_k1, peer_k2, peer_u,
              peer_v, norm_ple, w_ple_gate, w_ple_proj, norm_final):
    bp = x_prompt.shape[0]
    dt_ = x_prompt.dtype
    xp, xs = x_prompt, x_sample
    ssm_p, conv_p, wkv_p, shift_p = [], [], [], []
    ssm_s, conv_s, wkv_s, shift_s = [], [], [], []
    for i in range(DEPTH):
        lp = {
            'norm_mix': norm_mix[i], 'w_in': w_in[i], 'conv_w': conv_w[i], 'conv_b': conv_b[i],
            'dt_bias': dt_bias[i], 'a_log': a_log[i], 'd_skip': d_skip[i],
            'ssm_norm': ssm_norm[i], 'w_out_ssm': w_out_ssm[i], 'shift_mu': shift_mu[i],
            'decay_w0': decay_w0[i], 'decay_w2': decay_w2[i], 'aaa_a0': aaa_a0[i],
            'aaa_a2': aaa_a2[i], 'gate_g2': gate_g2[i], 'k_k': k_k[i], 'k_a': k_a[i],
            'r_k': r_k[i], 'lnx_g': lnx_g[i], 'lnx_b': lnx_b[i], 'w_out_rwkv': w_out_rwkv[i],
            'w_out': w_out[i], 'norm_ffn': norm_ffn[i], 'peer_wq': peer_wq[i],
            'peer_k1': peer_k1[i], 'peer_k2': peer_k2[i], 'peer_u': peer_u[i],
            'peer_v': peer_v[i], 'norm_ple': norm_ple[i], 'w_ple_gate': w_ple_gate[i],
            'w_ple_proj': w_ple_proj[i],
        }
        xp, s1, s2, s3, s4 = layer(
            xp, p_prompt[i],
            jnp.zeros((bp, CONV_WIDTH - 1, CONV_DIM), dt_),
            jnp.zeros((bp, SSM_HEADS, SSM_HEADDIM, SSM_STATE), dt_),
            jnp.zeros((bp, RWKV_HEADS, RWKV_HEADSIZE, RWKV_HEADSIZE), dt_),
            jnp.zeros((bp, SHIFT_DIM), dt_), lp)
        ssm_p.append(s1); conv_p.append(s2); wkv_p.append(s3); shift_p.append(s4)
        xs, t1, t2, t3, t4 = layer(xs, p_sample[i], state_conv[i], state_ssm[i], state_wkv[i],
                                   state_shift[i], lp)
        ssm_s.append(t1); conv_s.append(t2); wkv_s.append(t3); shift_s.append(t4)
    y_prompt = rmsnorm(xp, norm_final)
    y_sample = rmsnorm(xs, norm_final)
    return (y_prompt, y_sample, jnp.stack(ssm_p), jnp.stack(conv_p), jnp.stack(wkv_p),
            jnp.stack(shift_p), jnp.stack(ssm_s), jnp.stack(conv_s), jnp.stack(wkv_s),
            jnp.stack(shift_s))
```

```python
import numpy as np
import concourse.bass as bass
import concourse.mybir as mybir
from concourse.bass_utils import run_bass_kernel_spmd
from contextlib import ExitStack

F32 = mybir.dt.float32; BF16 = mybir.dt.bfloat16; I32 = mybir.dt.int32; U32 = mybir.dt.uint32
AF = mybir.ActivationFunctionType
ALU = mybir.AluOpType
AX = mybir.AxisListType

D = 1024; DIN = 2048; NH = 32; NG = 8; NST = 128; CONVD = 4096; SHIFT = 3328; IN_DIM = 11552
RH = 16; PLE = 256; NEXP = 16384
C_Z = 0; C_XBC = 2048; C_DT = 6144; C_RW = 6176; C_GM = 9504; C_GR = 10528


class Buf:
    __slots__ = ("name", "w", "r", "const")

    def __init__(self, name="", const=False):
        self.name = name; self.w = None; self.r = {}; self.const = const


class Ctx:
    ENG = ("pe", "act", "dve", "pool", "sp")
    DMAQ = {"sp": 8, "act": 2, "pool": 8}

    def __init__(self, nc):
        self.nc = nc; self.es = ExitStack()
        self.e = {"pe": nc.tensor, "act": nc.scalar, "dve": nc.vector, "pool": nc.gpsimd, "sp": nc.sync}
        self.sem = {k: self.es.enter_context(nc.semaphore("c_" + k)) for k in self.ENG}
        self.cnt = {k: 0 for k in self.ENG}
        self.seen = {k: {} for k in self.ENG}
        self.dsem = {}; self.dval = {}; self.dnext = {}
        for q, n in self.DMAQ.items():
            self.dsem[q] = [self.es.enter_context(nc.semaphore(f"d_{q}{i}")) for i in range(n)]
            self.dval[q] = [0] * n; self.dnext[q] = 0
        self.nwait = 0; self.ninst = 0
        self.ptr = 16640; self.hi = 0; self.uid = 0

    def sb(self, name, shape, dt):
        nb = int(np.prod(shape[1:])) * (2 if dt == BF16 else 4)
        nb = (nb + 31) // 32 * 32
        off = self.ptr; self.ptr += nb
        self.hi = max(self.hi, self.ptr)
        assert self.ptr <= 229000, f"SBUF overflow at {name}: {self.ptr}"
        self.uid += 1
        return self.nc.alloc_sbuf_tensor_at(f"s_{name}_{self.uid}", list(shape), dt, offset=off)

    def barrier(self):
        for e in self.ENG:
            for f in self.ENG:
                if f != e and self.cnt[f] > 0:
                    self._need(e, ("c", f, self.cnt[f], {}))
            for q in self.DMAQ:
                for i, sm in enumerate(self.dsem[q]):
                    if self.dval[q][i] > 0:
                        self._need(e, ("d", (q, i), sm, self.dval[q][i]))

    def ps(self, name, shape, dt=F32):
        return self.es.enter_context(self.nc.psum_tensor("p_" + name, list(shape), dt))

    def _need(self, eng, ev, raw=False):
        if ev is None:
            return
        if ev[0] == "c":
            _, f, n, snap = ev
            if f == eng and eng == "pe":
                return
            s = self.seen[eng]
            if s.get(f, 0) >= n:
                return
            self.e[eng].wait_ge(self.sem[f], n); self.nwait += 1
            s[f] = n
            for k, v in snap.items():
                if s.get(k, 0) < v:
                    s[k] = v
        else:
            _, key, semobj, val = ev
            if self.seen[eng].get(key, 0) >= val:
                return
            self.e[eng].wait_ge(semobj, val); self.nwait += 1
            self.seen[eng][key] = val

    def _deps(self, eng, r, w):
        for b in r:
            self._need(eng, b.w, True)
        for b in w:
            self._need(eng, b.w)
            for ev in b.r.values():
                self._need(eng, ev)

    def _commit(self, ev, r, w):
        key = ev[1]
        for b in r:
            if not b.const:
                b.r[key] = ev
        for b in w:
            b.w = ev; b.r = {}

    def op(self, eng, fn, r=(), w=()):
        self._deps(eng, r, w)
        ins = fn(self.e[eng])
        self.cnt[eng] += 1
        ins.then_inc(self.sem[eng], 1)
        ev = ("c", eng, self.cnt[eng], dict(self.seen[eng]))
        self._commit(ev, r, w); self.ninst += 1
        return ins

    def dma(self, q, out, in_, r=(), w=(), fn=None, **kw):
        self._deps(q, r, w)
        i = self.dnext[q]; n = len(self.dsem[q]); self.dnext[q] = (i + 1) % n
        semobj = self.dsem[q][i]; key = (q, i)
        if self.dval[q][i] > 0:
            self._need(q, ("d", key, semobj, self.dval[q][i]))
        ins = self.e[q].dma_start(out=out, in_=in_, **kw) if fn is None else fn(self.e[q])
        self.dval[q][i] += 16
        ins.then_inc(semobj, 16)
        ev = ("d", key, semobj, self.dval[q][i])
        self._commit(ev, r, w); self.ninst += 1
        return ins

    def wait_all_dma(self):
        for q in self.DMAQ:
            for i, s in enumerate(self.dsem[q]):
                if self.dval[q][i] > 0:
                    self._need("sp", ("d", (q, i), s, self.dval[q][i]))

    def close(self):
        self.es.close()


def build(SEQ, NS, do_peer=True, stop=99):
    NT = SEQ // 128
    nc = bass.Bass("TRN2", target_bir_lowering=False)
    c = Ctx(nc)

    def din(name, shape, dt=F32):
        return nc.dram_tensor(name, list(shape), dt, kind="ExternalInput").ap()

    def dout(name, shape, dt=F32):
        return nc.dram_tensor(name, list(shape), dt, kind="ExternalOutput").ap()

    def dscr(name, shape, dt=BF16):
        return nc.dram_tensor(name, list(shape), dt, kind="Internal").ap()

    xp = din("xp", [SEQ, D]); xs = din("xs", [NS, D]); pp = din("pp", [SEQ, PLE]); pps = din("pps", [NS, PLE])
    st_ssm = din("st_ssm", [NS, NH, 64, NST]); st_conv = din("st_conv", [NS, 3 * CONVD])
    st_wkv = din("st_wkv", [NS, RH, 64, 64]); st_shift = din("st_shift", [NS, SHIFT])
    w_in = din("w_in", [D, IN_DIM]); w_out_ssm = din("w_out_ssm", [DIN, D]); w_out_rwkv = din("w_out_rwkv", [D, D])
    w_out = din("w_out", [D, D]); peer_wq = din("peer_wq", [D, 2048]); w_ple_gate = din("w_ple_gate", [D, D])
    w_ple_proj = din("w_ple_proj", [PLE, D])
    loraw = din("loraw", [128, D]); loraa = din("loraa", [128, D]); gate_g2 = din("gate_g2", [128, D])
    k1T = din("k1T", [128, 8, 128]); k2T = din("k2T", [128, 8, 128])
    peer_u = din("peer_u", [NEXP, D]); peer_v = din("peer_v", [NEXP, D])
    NCST = 656; NCOL = 128 + 32 + 26 + 8 * 7 + 16; NROW = 96 + 4 * D
    cst_d = din("cst", [128, NCST]); cols_d = din("cols", [128, NCOL]); rows_d = din("rows", [128, NROW])

    yp = dout("yp", [SEQ, D]); ys = dout("ys", [NS, D])
    ssm_p = dout("ssm_p", [NH * 64, NST]); conv_p = dout("conv_p", [3, CONVD]); wkv_p = dout("wkv_p", [RH, 64, 64])
    shift_p = dout("shift_p", [1, SHIFT])
    ssm_s = dout("ssm_s", [NS, NH, 64, NST]); conv_s = dout("conv_s", [NS, 3 * CONVD])
    wkv_s = dout("wkv_s", [NS, RH, 64, 64]); shift_s = dout("shift_s", [NS, SHIFT])
    outb = Buf("outs")

    b_w_in = dscr("b_w_in", [D, IN_DIM]); b_w_out_ssm = dscr("b_w_out_ssm", [DIN, D])
    b_w_out_rwkv = dscr("b_w_out_rwkv", [D, D]); b_w_out = dscr("b_w_out", [D, D])
    b_wq = dscr("b_wq", [D, 2048]); b_w_ple_gate = dscr("b_w_ple_gate", [D, D])
    b_peer_u = dscr("b_peer_u", [NEXP, D]); b_peer_v = dscr("b_peer_v", [NEXP, D])

    V = lambda fn, r=(), w=(): c.op("dve", fn, r, w)
    A = lambda fn, r=(), w=(): c.op("act", fn, r, w)
    G = lambda fn, r=(), w=(): c.op("pool", fn, r, w)
    T = lambda fn, r=(), w=(): c.op("pe", fn, r, w)

    wbufs = {}

    def cast(name, dst, src, rows, step):
        bl = []
        for r0 in range(0, rows, step):
            b = Buf(name + str(r0), const=True)
            ncol = dst.shape[1]
            if ncol > 2048:
                c.dma("pool", dst[r0:r0 + step, :].rearrange("r (a b) -> r a b", a=8), src[r0:r0 + step, :].rearrange("r (a b) -> r a b", a=8), w=[b])
            else:
                c.dma("pool", dst[r0:r0 + step, :], src[r0:r0 + step, :], w=[b])
            bl.append(b)
        wbufs[name] = bl

    if stop <= -2:
        cast = lambda name, *a: wbufs.__setitem__(name, [])
    CH = 1444
    wch = []
    for j in range(8):
        b = Buf(f"w_in_c{j}", const=True)
        c.dma("pool", b_w_in[:, j * CH:(j + 1) * CH], w_in[:, j * CH:(j + 1) * CH], w=[b])
        wch.append(b)
    if stop <= -2:
        wch = []
    cast("w_out_ssm", b_w_out_ssm, w_out_ssm, DIN, 512)
    cast("w_out_rwkv", b_w_out_rwkv, w_out_rwkv, D, 512)
    cast("w_out", b_w_out, w_out, D, 512)
    cast("wq", b_wq, peer_wq, D, 256)
    cast("w_ple_gate", b_w_ple_gate, w_ple_gate, D, 512)
    if do_peer:
        cast("peer_u", b_peer_u, peer_u, NEXP, 1024)
        cast("peer_v", b_peer_v, peer_v, NEXP, 1024)

    cst = c.sb("cst", [128, NCST], F32); cols = c.sb("cols", [128, NCOL], F32); rows = c.sb("rows", [128, NROW], F32)
    bc = Buf("const", const=True)
    c.dma("sp", cst[:], cst_d[:], w=[bc]); c.dma("sp", cols[:], cols_d[:], w=[bc]); c.dma("sp", rows[:], rows_d[:], w=[bc])
    c.wait_all_dma()
    identf = cst[:, 0:128]; tri_le = cst[:, 128:256]; mask_gt = cst[:, 256:384]; onesf = cst[:, 384:512]
    blk64 = cst[:, 512:640]; iota16 = cst[:, 640:656]
    cstb = c.sb("cstb", [128, 640], BF16)
    V(lambda e: e.tensor_copy(cstb[:], cst[:, 0:640]), r=[bc], w=[bc])
    identb = cstb[:, 0:128]; onesb = cstb[:, 384:512]
    epsc = c.sb("epsc", [128, 4], F32)
    G(lambda e: e.memset(epsc[:, 0:1], 1e-6), w=[bc]); G(lambda e: e.memset(epsc[:, 1:2], 1e-12), w=[bc])
    G(lambda e: e.memset(epsc[:, 2:3], 64e-5), w=[bc]); G(lambda e: e.memset(epsc[:, 3:4], 1.0), w=[bc])
    o = 0
    convw = cols[:, o:o + 128]; o += 128
    convb = cols[:, o:o + 32]; o += 32
    mu_c = cols[:, o:o + 26]; o += 26
    w0_c = cols[:, o:o + 8]; o += 8
    a0_c = cols[:, o:o + 8]; o += 8
    kk_c = cols[:, o:o + 8]; o += 8
    ka_c = cols[:, o:o + 8]; o += 8
    rk_c = cols[:, o:o + 8]; o += 8
    lng_c = cols[:, o:o + 8]; o += 8
    lnb_c = cols[:, o:o + 8]; o += 8
    ssmn_c = cols[:, o:o + 16]; o += 16
    dtb_r = rows[:, 0:32]; alog_r = rows[:, 32:64]; dsk_r = rows[:, 64:96]
    g_mix = rows[:, 96:96 + D]; g_ffn = rows[:, 96 + D:96 + 2 * D]; g_ple = rows[:, 96 + 2 * D:96 + 3 * D]
    g_fin = rows[:, 96 + 3 * D:96 + 4 * D]
    Arow = c.sb("Arow", [128, 32], F32)
    A(lambda e: e.activation(out=Arow[:], in_=alog_r, func=AF.Exp), r=[bc], w=[bc])
    V(lambda e: e.tensor_scalar(Arow[:], Arow[:], -1.0, None, ALU.mult), r=[bc], w=[bc])

    lora_b = c.sb("lora_b", [128, D], BF16); loraa_b = c.sb("loraa_b", [128, D], BF16); g2_b = c.sb("g2_b", [128, D], BF16)
    plew = c.sb("plew", [128, 2, D], BF16); k1b = c.sb("k1b", [128, 8, 128], BF16); k2b = c.sb("k2b", [128, 8, 128], BF16)
    _mark = c.ptr
    c.ptr = 170016
    stage = c.sb("stage", [128, 2, D], F32); bstage = Buf("stage")
    c.ptr = _mark
    c.dma("sp", stage[:, 0, :], loraw[:], w=[bstage]); c.dma("sp", stage[:, 1, :], gate_g2[:], w=[bstage])
    V(lambda e: e.tensor_copy(lora_b[:], stage[:, 0, :]), r=[bstage], w=[bc])
    V(lambda e: e.tensor_copy(g2_b[:], stage[:, 1, :]), r=[bstage], w=[bc])
    c.dma("sp", stage[:, 0, :], loraa[:], w=[bstage])
    V(lambda e: e.tensor_copy(loraa_b[:], stage[:, 0, :]), r=[bstage], w=[bc])
    c.dma("sp", stage[:], w_ple_proj.rearrange("(kc p) n -> p kc n", p=128), w=[bstage])
    V(lambda e: e.tensor_copy(plew[:], stage[:]), r=[bstage], w=[bc])
    c.dma("sp", stage[:, 0, :], k1T.rearrange("p h n -> p (h n)"), w=[bstage])
    c.dma("sp", stage[:, 1, :], k2T.rearrange("p h n -> p (h n)"), w=[bstage])
    V(lambda e: e.tensor_copy(k1b[:].rearrange("p h n -> p (h n)"), stage[:, 0, :]), r=[bstage], w=[bc])
    V(lambda e: e.tensor_copy(k2b[:].rearrange("p h n -> p (h n)"), stage[:, 1, :]), r=[bstage], w=[bc])

    NPS = 6
    pst = [c.ps(f"ps{i}", [128, 512], F32) for i in range(NPS)]
    psb = [Buf(f"ps{i}") for i in range(NPS)]
    psn = [0]

    def PS():
        i = psn[0] % NPS; psn[0] += 1
        return pst[i], psb[i]

    pstb = [c.ps(f"psb{i}", [128, 1024], BF16) for i in range(2)]
    psbb = [Buf(f"psb{i}") for i in range(2)]
    psbn = [0]

    def PSB():
        i = psbn[0] % 2; psbn[0] += 1
        return pstb[i], psbb[i]

    NW = 3
    wt = [c.sb(f"wt{i}", [128, 8, 512], BF16) for i in range(NW)]
    wtb = [Buf(f"wt{i}") for i in range(NW)]
    wn = [0]

    def WL(name, src, k0, c0, cw):
        i = wn[0] % NW; wn[0] += 1
        s = src[k0 * 128:(k0 + 8) * 128, c0:c0 + cw].rearrange("(kc p) n -> p kc n", p=128)
        c.dma("sp", wt[i][:, :, 0:cw], s, r=(wch[c0 // CH:(c0 + cw - 1) // CH + 1] if name == "w_in" else wbufs[name]), w=[wtb[i]])
        return wt[i], wtb[i]

    def SB(name, shape, dt=F32):
        return c.sb(name, shape, dt), Buf(name)

    xres, bxres = SB("xres", [128, D])
    h32, bh32 = SB("h32", [128, D])
    hb, bhb = SB("hb", [128, D], BF16)
    hT, bhT = SB("hT", [128, 8, 128], BF16)
    ss, bss = SB("ss", [128, 8])
    junk, bjunk = SB("junk", [128, D])
    zs, bzs = SB("zs", [128, DIN])
    raw, braw_ = SB("raw", [128, 4, 132]); rawn = [0]; braws = [Buf(f"raw{i}") for i in range(4)]
    acc, bacc_ = SB("acc", [128, 4, 128]); baccs = [Buf(f"acc{i}") for i in range(4)]
    carry, bcarry = SB("carry", [128, 32, 3])
    carry_rw, bcarry_rw = SB("carry_rw", [128, 26])
    dts, bdts = SB("dts", [128, 8, 32])
    cdb, bcdb = SB("cdb", [128, 32])
    hst, bhst = SB("hst", [128, DIN])
    hstb, bhstb = SB("hstb", [128, DIN], BF16)
    rwT, brwT = SB("rwT", [128, 26, 128])
    gmT, bgmT = SB("gmT", [128, 8, 128]); grT, bgrT = SB("grT", [128, 8, 128])
    rowbuf, browbuf = SB("rowbuf", [16, 2, 512]); rown = [0]
    m1, bm1 = SB("m1", [128, 8, 128])
    mixT, bmixT = SB("mixT", [128, 8, 128], BF16)
    Sst, bSst = SB("Sst", [128, 512])
    lsegn = [0]
    eidx, beidx = SB("eidx", [128, 128], I32)
    gatew, bgatew = SB("gatew", [128, 128])
    ARENA0 = c.ptr
    NGB = 6

    G(lambda e: e.memset(hst[:], 0.0), w=[bhst]); G(lambda e: e.memset(hstb[:], 0.0), w=[bhstb])
    G(lambda e: e.memset(Sst[:], 0.0), w=[bSst]); G(lambda e: e.memset(carry[:], 0.0), w=[bcarry])
    G(lambda e: e.memset(carry_rw[:], 0.0), w=[bcarry_rw]); G(lambda e: e.memset(eidx[:], 0), w=[beidx])
    G(lambda e: e.memset(gatew[:], 0.0), w=[bgatew])
    c.barrier()

    def bfv(ps_tile):
        return ps_tile[:].bitcast(BF16)

    def rmsnorm_T(P, grow):
        G(lambda e: e.memset(ss[:P, 0:1], 0.0), w=[bss])
        A(lambda e: e.activation(out=junk[:P, :], in_=xres[:P, :], func=AF.Square, accum_out=ss[:P, 0:1]), r=[bxres, bss], w=[bjunk, bss])
        A(lambda e: e.activation(out=ss[:P, 1:2], in_=ss[:P, 0:1], func=AF.Sqrt, scale=1.0 / D, bias=epsc[:P, 0:1]), r=[bss, bc], w=[bss])
        V(lambda e: e.reciprocal(ss[:P, 2:3], ss[:P, 1:2]), r=[bss], w=[bss])
        V(lambda e: e.scalar_tensor_tensor(out=h32[:P, :], in0=xres[:P, :], scalar=ss[:P, 2:3], in1=grow[:P, :], op0=ALU.mult, op1=ALU.mult), r=[bxres, bss, bc], w=[bh32])

    def to_hT(P):
        A(lambda e: e.activation(out=hb[:P, :], in_=h32[:P, :], func=AF.Copy), r=[bh32], w=[bhb])
        p, bp = PSB(); pv = p[:, :]
        for k in range(8):
            T(lambda e: e.transpose(pv[:, k * 128:k * 128 + P], hb[:P, k * 128:(k + 1) * 128], identb[:P, :P]), r=[bhb, bc], w=[bp])
        A(lambda e: e.activation(out=hT[:, :, :P], in_=pv.rearrange("p (k t) -> p k t", k=8)[:, :, :P], func=AF.Copy), r=[bp], w=[bhT])

    def mm_tm(P, wtile, bw, cw, lhs, blhs, nk=8, kofs=0, first=True, last=True, pp_=None):
        if pp_ is None:
            pp_ = PS()
        p, bp = pp_
        for k in range(nk):
            T(lambda e: e.matmul(p[:P, :cw], lhs[:, kofs + k, :P], wtile[:, k, :cw], start=(first and k == 0), stop=(last and k == nk - 1)), r=[blhs, bw], w=[bp])
        return p, bp

    def mm_fm(P, wtile, bw, j, rhs, brhs, nk=8, kofs=0, first=True, last=True, pp_=None):
        if pp_ is None:
            pp_ = PS()
        p, bp = pp_
        for k in range(nk):
            T(lambda e: e.matmul(p[:, :P], wtile[:, k, j * 128:(j + 1) * 128], rhs[:, kofs + k, :P], start=(first and k == 0), stop=(last and k == nk - 1)), r=[brhs, bw], w=[bp])
        return p, bp

    def tile_step(ti):
        sample = ti == NT
        P = NS if sample else 128
        lastp = (ti == NT - 1)
        c.barrier(); c.ptr = ARENA0
        if sample:
            histT, bhistT = SB("histT", [128, 32, 3, 16])
            shT, bshT = SB("shT", [128, 26, 16])
            C_tm, bC_tm = SB("C_tm", [128, 1024], BF16)
            dtAx, bdtAx = SB("dtAx", [16, 128]); Bdg, bBdg = SB("Bdg", [16, 16 * 128], BF16); Cdg, bCdg = SB("Cdg", [16, 16 * 128], BF16)
            sst, bsst = SB("sst", [128, 16, 128]); decs, bdecs = SB("decs", [128, 16]); yTs, byTs = SB("yTs", [128, 16, 16])
            sc1, bsc1 = SB("sc1", [128, 512])
        else:
            xw, bxw = SB("xw", [128, DIN], BF16)
            CBm, bCBm = SB("CBm", [128, 128])
            lseg, blseg_ = SB("lseg", [128, 4, 128]); blsegs = [Buf(f"lseg{i}") for i in range(4)]
            Eseg, bEseg_ = SB("Eseg", [128, 4, 128]); bEsegs = [Buf(f"Eseg{i}") for i in range(4)]
            MT, bMT_ = SB("MT", [128, 4, 128], BF16); bMTs = [Buf(f"MT{i}") for i in range(4)]
        xcT, bxcT = SB("xcT", [128, 16, 128], BF16)
        BT, bBT = SB("BT", [128, 8, 128], BF16)
        CT, bCT = SB("CT", [128, 8, 128], BF16)
        x_tm, bx_tm = SB("x_tm", [128, DIN], BF16)
        B_tm, bB_tm = SB("B_tm", [128, 1024], BF16)
        xdt, bxdt = SB("xdt", [128, DIN], BF16)
        ysb, bysb = SB("ysb", [128, DIN])
        ytmp, bytmp = SB("ytmp", [128, DIN])
        ynb, bynb = SB("ynb", [128, DIN], BF16)
        ynT, bynT = SB("ynT", [128, 16, 128], BF16)
        rows_sel = None
        if sample:
            rows_sel = (0, NS)
        elif lastp:
            rows_sel = (125, 128)
        if sample:
            c.dma("sp", xres[:P, :], xs[:, :], w=[bxres])
        else:
            c.dma("sp", xres[:P, :], xp[ti * 128:(ti + 1) * 128, :], w=[bxres])
        rmsnorm_T(P, g_mix); to_hT(P)
        if sample:
            for k3 in range(3):
                for q in range(4):
                    c.dma("sp", junk[:NS, :], st_conv[:, k3 * CONVD + q * 1024: k3 * CONVD + (q + 1) * 1024], w=[bjunk])
                    p, bp = PS()
                    for j in range(8):
                        T(lambda e: e.transpose(p[:, j * 16:j * 16 + NS], junk[:NS, j * 128:(j + 1) * 128], identf[:NS, :NS]), r=[bjunk, bc], w=[bp])
                    A(lambda e: e.activation(out=histT[:, q * 8:(q + 1) * 8, k3, :NS], in_=p[:, 0:128].rearrange("p (j s) -> p j s", j=8)[:, :, :NS], func=AF.Copy), r=[bp], w=[bhistT])
            for q in range(4):
                cw_ = min(1024, SHIFT - q * 1024)
                nj = cw_ // 128
                c.dma("sp", junk[:NS, :cw_], st_shift[:, q * 1024:q * 1024 + cw_], w=[bjunk])
                p, bp = PS()
                for j in range(nj):
                    T(lambda e: e.transpose(p[:, j * 16:j * 16 + NS], junk[:NS, j * 128:(j + 1) * 128], identf[:NS, :NS]), r=[bjunk, bc], w=[bp])
                A(lambda e: e.activation(out=shT[:, q * 8:q * 8 + nj, :NS], in_=p[:, 0:nj * 16].rearrange("p (j s) -> p j s", j=nj)[:, :, :NS], func=AF.Copy), r=[bp], w=[bshT])
            c.dma("sp", conv_s[:, 0:2 * CONVD], st_conv[:, CONVD:3 * CONVD], w=[outb])

        def lastrows(wtile, bw, cw, dst_fn):
            if rows_sel is None:
                return
            r0, r1 = rows_sel; M = r1 - r0
            p, bp = PS()
            for k in range(8):
                T(lambda e: e.matmul(p[:M, :cw], hT[:, k, r0:r1], wtile[:, k, :cw], start=(k == 0), stop=(k == 7)), r=[bhT, bw], w=[bp])
            i = rown[0] % 2; rown[0] += 1
            A(lambda e: e.activation(out=rowbuf[:M, i, :cw], in_=p[:M, :cw], func=AF.Copy), r=[bp], w=[browbuf])
            dst_fn(rowbuf, i, M)

        if stop < 1:
            return
        for b4 in range(4):
            w_, bw = WL("w_in", b_w_in, 0, C_Z + b4 * 512, 512)
            p, bp = mm_tm(P, w_, bw, 512, hT, bhT)
            A(lambda e: e.activation(out=zs[:P, b4 * 512:(b4 + 1) * 512], in_=p[:P, :], func=AF.Silu), r=[bp], w=[bzs])
        for b8 in range(8):
            w_, bw = WL("w_in", b_w_in, 0, C_XBC + b8 * 512, 512)
            for j in range(4):
                ct = b8 * 4 + j
                p, bp = mm_fm(P, w_, bw, j, hT, bhT)
                ri = rawn[0] % 4; rawn[0] += 1; braw = braws[ri]; bacc = baccs[ri]
                if not sample:
                    A(lambda e: e.activation(out=raw[:, ri, 3:3 + P], in_=p[:, :P], func=AF.Copy), r=[bp], w=[braw])
                    V(lambda e: e.tensor_copy(raw[:, ri, 0:3], carry[:, ct, :]), r=[bcarry], w=[braw])
                    taps = [raw[:, ri, k:k + P] for k in range(4)]
                    V(lambda e: e.tensor_copy(carry[:, ct, :], raw[:, ri, P:P + 3]), r=[braw], w=[bcarry])
                    rb = [braw]
                else:
                    A(lambda e: e.activation(out=raw[:, ri, 0:P], in_=p[:, :P], func=AF.Copy), r=[bp], w=[braw])
                    taps = [histT[:, ct, 0, :P], histT[:, ct, 1, :P], histT[:, ct, 2, :P], raw[:, ri, 0:P]]
                    rb = [braw, bhistT]
                V(lambda e: e.tensor_scalar(acc[:, ri, :P], taps[0], convw[:, ct * 4:ct * 4 + 1], None, ALU.mult), r=rb + [bc], w=[bacc])
                for k in range(1, 4):
                    V(lambda e: e.scalar_tensor_tensor(out=acc[:, ri, :P], in0=taps[k], scalar=convw[:, ct * 4 + k:ct * 4 + k + 1], in1=acc[:, ri, :P], op0=ALU.mult, op1=ALU.add), r=rb + [bc, bacc], w=[bacc])
                if ct < 16:
                    dst, bd = xcT[:, ct, :P], bxcT
                elif ct < 24:
                    dst, bd = BT[:, ct - 16, :P], bBT
                else:
                    dst, bd = CT[:, ct - 24, :P], bCT
                A(lambda e: e.activation(out=dst, in_=acc[:, ri, :P], func=AF.Silu, bias=convb[:, ct:ct + 1]), r=[bacc, bc], w=[bd])

            def dstc(rbuf, i, M, b8=b8):
                if sample:
                    c.dma("sp", conv_s[:, 2 * CONVD + b8 * 512:2 * CONVD + (b8 + 1) * 512], rbuf[:M, i, :], r=[browbuf], w=[outb])
                else:
                    c.dma("sp", conv_p[:, b8 * 512:(b8 + 1) * 512], rbuf[:M, i, :], r=[browbuf], w=[outb])
            lastrows(w_, bw, 512, dstc)
        w_, bw = WL("w_in", b_w_in, 0, C_DT, 32)
        p, bp = mm_tm(P, w_, bw, 32, hT, bhT)
        V(lambda e: e.tensor_tensor(dts[:P, 0, :], p[:P, 0:32], dtb_r[:P, :], ALU.add), r=[bp, bc], w=[bdts])
        A(lambda e: e.activation(out=dts[:P, 1, :], in_=dts[:P, 0, :], func=AF.Abs), r=[bdts], w=[bdts])
        A(lambda e: e.activation(out=dts[:P, 1, :], in_=dts[:P, 1, :], func=AF.Exp, scale=-1.0), r=[bdts], w=[bdts])
        A(lambda e: e.activation(out=dts[:P, 1, :], in_=dts[:P, 1, :], func=AF.Ln, bias=epsc[:P, 3:4]), r=[bdts, bc], w=[bdts])
        V(lambda e: e.scalar_tensor_tensor(out=dts[:P, 2, :], in0=dts[:P, 0, :], scalar=0.0, in1=dts[:P, 1, :], op0=ALU.max, op1=ALU.add), r=[bdts], w=[bdts])
        V(lambda e: e.tensor_tensor(dts[:P, 3, :], dts[:P, 2, :], Arow[:P, :], ALU.mult), r=[bdts, bc], w=[bdts])
        for b7 in range(7):
            cw = 512 if b7 < 6 else 256
            w_, bw = WL("w_in", b_w_in, 0, C_RW + b7 * 512, cw)
            for j in range(cw // 128):
                ct = b7 * 4 + j
                p, bp = mm_fm(P, w_, bw, j, hT, bhT)
                ri = rawn[0] % 4; rawn[0] += 1; braw = braws[ri]; bacc = baccs[ri]
                if not sample:
                    A(lambda e: e.activation(out=raw[:, ri, 1:1 + P], in_=p[:, :P], func=AF.Copy), r=[bp], w=[braw])
                    V(lambda e: e.tensor_copy(raw[:, ri, 0:1], carry_rw[:, ct:ct + 1]), r=[bcarry_rw], w=[braw])
                    prev = raw[:, ri, 0:P]; cur = raw[:, ri, 1:1 + P]
                    V(lambda e: e.tensor_copy(carry_rw[:, ct:ct + 1], raw[:, ri, P:P + 1]), r=[braw], w=[bcarry_rw])
                    rb = [braw]
                else:
                    A(lambda e: e.activation(out=raw[:, ri, 0:P], in_=p[:, :P], func=AF.Copy), r=[bp], w=[braw])
                    prev = shT[:, ct, :P]; cur = raw[:, ri, 0:P]
                    rb = [braw, bshT]
                V(lambda e: e.tensor_tensor(acc[:, ri, :P], prev, cur, ALU.subtract), r=rb, w=[bacc])
                V(lambda e: e.scalar_tensor_tensor(out=rwT[:, ct, :P], in0=acc[:, ri, :P], scalar=mu_c[:, ct:ct + 1], in1=cur, op0=ALU.mult, op1=ALU.add), r=rb + [bacc, bc], w=[brwT])

            def dsts(rbuf, i, M, b7=b7, cw=cw):
                if sample:
                    c.dma("sp", shift_s[:, b7 * 512:b7 * 512 + cw], rbuf[:M, i, :cw], r=[browbuf], w=[outb])
                else:
                    c.dma("sp", shift_p[:, b7 * 512:b7 * 512 + cw], rbuf[2:3, i, :cw], r=[browbuf], w=[outb])
            lastrows(w_, bw, cw, dsts)
        for gi, (c0, dstT, bdT) in enumerate(((C_GM, gmT, bgmT), (C_GR, grT, bgrT))):
            for b2 in range(2):
                w_, bw = WL("w_in", b_w_in, 0, c0 + b2 * 512, 512)
                for j in range(4):
                    p, bp = mm_fm(P, w_, bw, j, hT, bhT)
                    A(lambda e: e.activation(out=dstT[:, b2 * 4 + j, :P], in_=p[:, :P], func=AF.Sigmoid), r=[bp], w=[bdT])

        if stop < 2:
            return
        for half in range(2):
            p, bp = PSB(); pv = p[:, :]
            for j in range(8):
                T(lambda e: e.transpose(pv[:P, j * 128:(j + 1) * 128], xcT[:, half * 8 + j, :P], identb[:, :]), r=[bxcT, bc], w=[bp])
            A(lambda e: e.activation(out=x_tm[:P, half * 1024:(half + 1) * 1024], in_=pv[:P, :], func=AF.Copy), r=[bp], w=[bx_tm])
        p, bp = PSB(); pv = p[:, :]
        for j in range(8):
            T(lambda e: e.transpose(pv[:P, j * 128:(j + 1) * 128], BT[:, j, :P], identb[:, :]), r=[bBT, bc], w=[bp])
        A(lambda e: e.activation(out=B_tm[:P, :], in_=pv[:P, :], func=AF.Copy), r=[bp], w=[bB_tm])
        x3 = x_tm[:P, :].rearrange("p (h d) -> p h d", h=NH)
        y3 = ysb[:P, :].rearrange("p (h d) -> p h d", h=NH)
        yt3 = ytmp[:P, :].rearrange("p (h d) -> p h d", h=NH)

        def bc64(ap2):
            return ap2.unsqueeze(2).to_broadcast([P, NH, 64])
        if not sample:
            p, bp = PS()
            T(lambda e: e.matmul(p[:P, 0:32], tri_le[:P, :P], dts[:P, 3, :], start=True, stop=True), r=[bdts, bc], w=[bp])
            T(lambda e: e.matmul(p[:P, 32:64], mask_gt[:P, :P], dts[:P, 3, :], start=True, stop=True), r=[bdts, bc], w=[bp])
            T(lambda e: e.matmul(p[:, 64:96], onesf[:P, :], dts[:P, 3, :], start=True, stop=True), r=[bdts, bc], w=[bp])
            A(lambda e: e.activation(out=dts[:P, 4, :], in_=p[:P, 0:32], func=AF.Exp), r=[bp], w=[bdts])
            A(lambda e: e.activation(out=dts[:P, 5, :], in_=p[:P, 32:64], func=AF.Exp), r=[bp], w=[bdts])
            A(lambda e: e.activation(out=cdb[:, :], in_=p[:, 64:96], func=AF.Exp), r=[bp], w=[bcdb])
            V(lambda e: e.tensor_tensor(dts[:P, 6, :], dts[:P, 5, :], dts[:P, 2, :], ALU.mult), r=[bdts], w=[bdts])
            V(lambda e: e.tensor_tensor(xdt[:P, :].rearrange("p (h d) -> p h d", h=NH), x3, bc64(dts[:P, 2, :]), ALU.mult), r=[bx_tm, bdts], w=[bxdt])
            V(lambda e: e.tensor_tensor(xw[:P, :].rearrange("p (h d) -> p h d", h=NH), x3, bc64(dts[:P, 6, :]), ALU.mult), r=[bx_tm, bdts], w=[bxw])
            for g in range(NG):
                p, bp = PS()
                T(lambda e: e.matmul(p[:P, :P], BT[:, g, :P], CT[:, g, :P], start=True, stop=True), r=[bBT, bCT], w=[bp])
                V(lambda e: e.tensor_tensor(CBm[:P, :P], p[:P, :P], tri_le[:P, :P], ALU.mult), r=[bp, bc], w=[bCBm])
                py, bpy = PS()
                po, bpo = PS()
                for r4 in range(4):
                    h = g * 4 + r4
                    li = lsegn[0] % 4; lsegn[0] += 1; blseg = blsegs[li]; bEseg = bEsegs[li]; bMT = bMTs[li]
                    V(lambda e: e.tensor_scalar(lseg[:P, li, :P], mask_gt[:P, :P], dts[:P, 3, h:h + 1], None, ALU.mult), r=[bdts, bc], w=[blseg])
                    p2, bp2 = PS()
                    T(lambda e: e.matmul(p2[:P, :P], lseg[:P, li, :P], tri_le[:P, :P], start=True, stop=True), r=[blseg, bc], w=[bp2])
                    A(lambda e: e.activation(out=Eseg[:P, li, :P], in_=p2[:P, :P], func=AF.Exp), r=[bp2], w=[bEseg])
                    V(lambda e: e.tensor_tensor(MT[:P, li, :P], Eseg[:P, li, :P], CBm[:P, :P], ALU.mult), r=[bEseg, bCBm], w=[bMT])
                    T(lambda e: e.matmul(py[:P, r4 * 64:(r4 + 1) * 64], MT[:P, li, :P], xdt[:P, h * 64:(h + 1) * 64], start=True, stop=True), r=[bMT, bxdt], w=[bpy])
                    T(lambda e: e.matmul(po[:P, r4 * 64:(r4 + 1) * 64], CT[:, g, :P], hstb[:, h * 64:(h + 1) * 64], start=True, stop=True), r=[bCT, bhstb], w=[bpo])
                A(lambda e: e.activation(out=ysb[:P, g * 256:(g + 1) * 256], in_=py[:P, 0:256], func=AF.Copy), r=[bpy], w=[bysb])
                V(lambda e: e.tensor_tensor(yt3[:, g * 4:(g + 1) * 4, :], po[:P, 0:256].rearrange("p (h d) -> p h d", h=4), dts[:P, 4, g * 4:(g + 1) * 4].unsqueeze(2).to_broadcast([P, 4, 64]), ALU.mult), r=[bpo, bdts], w=[bytmp])
                V(lambda e: e.tensor_tensor(ysb[:P, g * 256:(g + 1) * 256], ysb[:P, g * 256:(g + 1) * 256], ytmp[:P, g * 256:(g + 1) * 256], ALU.add), r=[bysb, bytmp], w=[bysb])
            for q in range(4):
                p, bp = PS()
                for r8 in range(8):
                    h = q * 8 + r8; g = h // 4
                    T(lambda e: e.matmul(p[:, r8 * 64:(r8 + 1) * 64], B_tm[:P, g * 128:(g + 1) * 128], xw[:P, h * 64:(h + 1) * 64], start=True, stop=True), r=[bB_tm, bxw], w=[bp])
                hq = hst[:, q * 512:(q + 1) * 512]
                V(lambda e: e.tensor_tensor(hq.rearrange("p (h d) -> p h d", h=8), hq.rearrange("p (h d) -> p h d", h=8), cdb[:, q * 8:(q + 1) * 8].unsqueeze(2).to_broadcast([128, 8, 64]), ALU.mult), r=[bhst, bcdb], w=[bhst])
                V(lambda e: e.tensor_tensor(hq, hq, p[:, :], ALU.add), r=[bhst, bp], w=[bhst])
            A(lambda e: e.activation(out=hstb[:, :], in_=hst[:, :], func=AF.Copy), r=[bhst], w=[bhstb])
            if lastp:
                for r16 in range(16):
                    p, bp = PS()
                    T(lambda e: e.transpose(p[:, 0:128], hst[:, r16 * 128:(r16 + 1) * 128], identf[:, :]), r=[bhst, bc], w=[bp])
                    i = rawn[0] % 4; rawn[0] += 1; bacc = baccs[i]
                    A(lambda e: e.activation(out=acc[:, i, :], in_=p[:, 0:128], func=AF.Copy), r=[bp], w=[bacc])
                    c.dma("sp", ssm_p[r16 * 128:(r16 + 1) * 128, :], acc[:, i, :], r=[bacc], w=[outb])
        else:
            p, bp = PSB(); pv = p[:, :]
            for j in range(8):
                T(lambda e: e.transpose(pv[:P, j * 128:(j + 1) * 128], CT[:, j, :P], identb[:, :]), r=[bCT, bc], w=[bp])
            A(lambda e: e.activation(out=C_tm[:P, :], in_=pv[:P, :], func=AF.Copy), r=[bp], w=[bC_tm])
            V(lambda e: e.tensor_tensor(xdt[:P, :].rearrange("p (h d) -> p h d", h=NH), x3, bc64(dts[:P, 2, :]), ALU.mult), r=[bx_tm, bdts], w=[bxdt])
            for r16 in range(16):
                g = r16 // 2
                if r16 % 2 == 0:
                    V(lambda e: e.tensor_tensor(Bdg[:P, :].rearrange("p (s n) -> p s n", s=16), B_tm[:P, g * 128:(g + 1) * 128].unsqueeze(1).to_broadcast([P, 16, 128]), identf[:P, 0:16].unsqueeze(2).to_broadcast([P, 16, 128]), ALU.mult), r=[bB_tm, bc], w=[bBdg])
                    V(lambda e: e.tensor_tensor(Cdg[:P, :].rearrange("p (s n) -> p s n", s=16), C_tm[:P, g * 128:(g + 1) * 128].unsqueeze(1).to_broadcast([P, 16, 128]), identf[:P, 0:16].unsqueeze(2).to_broadcast([P, 16, 128]), ALU.mult), r=[bC_tm, bc], w=[bCdg])
                src = st_ssm.rearrange("s h p n -> (h p) s n")[r16 * 128:(r16 + 1) * 128, :, :]
                c.dma("sp", sst[:, :NS, :], src, w=[bsst])
                V(lambda e: e.tensor_copy(dtAx[:P, :].rearrange("p (h d) -> p h d", h=2), dts[:P, 3, 2 * r16:2 * r16 + 2].unsqueeze(2).to_broadcast([P, 2, 64])), r=[bdts], w=[bdtAx])
                p, bp = PS()
                T(lambda e: e.matmul(p[:, 0:NS], dtAx[:P, :], identf[:P, :NS], start=True, stop=True), r=[bdtAx, bc], w=[bp])
                A(lambda e: e.activation(out=decs[:, :NS], in_=p[:, 0:NS], func=AF.Exp), r=[bp], w=[bdecs])
                for q in range(NS // 4):
                    po, bpo = PS()
                    T(lambda e: e.matmul(po[:, :], xdt[:P, r16 * 128:(r16 + 1) * 128], Bdg[:P, q * 512:(q + 1) * 512], start=True, stop=True), r=[bxdt, bBdg], w=[bpo])
                    pc, bpc = PS()
                    T(lambda e: e.matmul(pc[:, :], onesb[:P, :], Cdg[:P, q * 512:(q + 1) * 512], start=True, stop=True), r=[bCdg, bc], w=[bpc])
                    sq = sst[:, q * 4:(q + 1) * 4, :]
                    V(lambda e: e.tensor_tensor(sq, sq, decs[:, q * 4:(q + 1) * 4].unsqueeze(2).to_broadcast([128, 4, 128]), ALU.mult), r=[bsst, bdecs], w=[bsst])
                    V(lambda e: e.tensor_tensor(sq, sq, po[:, :].rearrange("p (s n) -> p s n", s=4), ALU.add), r=[bsst, bpo], w=[bsst])
                    V(lambda e: e.tensor_tensor(sc1[:, :].rearrange("p (s n) -> p s n", s=4), sq, pc[:, :].rearrange("p (s n) -> p s n", s=4), ALU.mult), r=[bsst, bpc], w=[bsc1])
                    V(lambda e: e.tensor_reduce(yTs[:, r16, q * 4:(q + 1) * 4], sc1[:, :].rearrange("p (s n) -> p s n", s=4), AX.X, ALU.add), r=[bsc1], w=[byTs])
                dstd = ssm_s.rearrange("s h p n -> (h p) s n")[r16 * 128:(r16 + 1) * 128, :, :]
                c.dma("sp", dstd, sst[:, :NS, :], r=[bsst], w=[outb])
            for q in range(4):
                p, bp = PS()
                for j in range(4):
                    T(lambda e: e.transpose(p[:NS, j * 128:(j + 1) * 128], yTs[:, q * 4 + j, :NS], identf[:, :]), r=[byTs, bc], w=[bp])
                A(lambda e: e.activation(out=ysb[:P, q * 512:(q + 1) * 512], in_=p[:P, :], func=AF.Copy), r=[bp], w=[bysb])
        V(lambda e: e.tensor_tensor(yt3, x3, bc64(dsk_r[:P, :]), ALU.mult), r=[bx_tm, bc], w=[bytmp])
        V(lambda e: e.tensor_tensor(ysb[:P, :], ysb[:P, :], ytmp[:P, :], ALU.add), r=[bysb, bytmp], w=[bysb])
        V(lambda e: e.tensor_tensor(ysb[:P, :], ysb[:P, :], zs[:P, :], ALU.mult), r=[bysb, bzs], w=[bysb])
        V(lambda e: e.tensor_tensor(ytmp[:P, :], ysb[:P, :], ysb[:P, :], ALU.mult), r=[bysb], w=[bytmp])
        V(lambda e: e.tensor_reduce(ss[:P, 0:8], ytmp[:P, :].rearrange("p (g d) -> p g d", g=8), AX.X, ALU.add), r=[bytmp], w=[bss])
        A(lambda e: e.activation(out=ss[:P, 0:8], in_=ss[:P, 0:8], func=AF.Sqrt, scale=1.0 / 256, bias=epsc[:P, 0:1]), r=[bss, bc], w=[bss])
        V(lambda e: e.reciprocal(ss[:P, 0:8], ss[:P, 0:8]), r=[bss], w=[bss])
        V(lambda e: e.tensor_tensor(ynb[:P, :].rearrange("p (g d) -> p g d", g=8), ysb[:P, :].rearrange("p (g d) -> p g d", g=8), ss[:P, 0:8].unsqueeze(2).to_broadcast([P, 8, 256]), ALU.mult), r=[bysb, bss], w=[bynb])
        for half in range(2):
            p, bp = PSB(); pv = p[:, :]
            for j in range(8):
                T(lambda e: e.transpose(pv[:, j * 128:j * 128 + P], ynb[:P, (half * 8 + j) * 128:(half * 8 + j + 1) * 128], identb[:P, :P]), r=[bynb, bc], w=[bp])
            V(lambda e: e.tensor_tensor(ynT[:, half * 8:(half + 1) * 8, :P], pv.rearrange("p (k t) -> p k t", k=8)[:, :, :P], ssmn_c[:, half * 8:(half + 1) * 8].unsqueeze(2).to_broadcast([128, 8, P]), ALU.mult), r=[bp, bc], w=[bynT])
        for b2 in range(2):
            wa, bwa = WL("w_out_ssm", b_w_out_ssm, 0, b2 * 512, 512)
            wb_, bwb = WL("w_out_ssm", b_w_out_ssm, 8, b2 * 512, 512)
            for j in range(4):
                pp_ = PS()
                mm_fm(P, wa, bwa, j, ynT, bynT, first=True, last=False, pp_=pp_)
                p, bp = mm_fm(P, wb_, bwb, j, ynT, bynT, kofs=8, first=False, last=True, pp_=pp_)
                V(lambda e: e.tensor_tensor(m1[:, b2 * 4 + j, :P], p[:, :P], gmT[:, b2 * 4 + j, :P], ALU.mult), r=[bp, bgmT], w=[bm1])

        if stop < 3:
            return
        c.barrier(); c.ptr = ARENA0
        lin, blin = SB("lin", [128, 2, 128], BF16)
        dec, bdec = SB("dec", [128, 8, 128]); asg, basg = SB("asg", [128, 8, 128]); gT, bgT = SB("gT", [128, 8, 128])
        kkn, bkkn = SB("kkn", [128, 8, 128]); kp, bkp = SB("kp", [128, 8, 128]); t1, bt1 = SB("t1", [128, 8, 128])
        t2, bt2 = SB("t2", [128, 8, 128])
        fmb, bfmb = SB("fmb", [128, 6, 8, 128], BF16)
        tmv, btmv = SB("tmv", [128, 6, D], BF16)
        if sample:
            Ssmp, bSsmp0 = SB("Ssmp", [128, 2, 512]); bSsmp = [Buf("S0"), Buf("S1")]
        sc1, bsc1 = SB("sc1", [128, 512]); sc2, bsc2 = SB("sc2", [128, 512]); sc3, bsc3 = SB("sc3", [128, 512])
        sc4, bsc4_ = SB("sc4", [128, 1, 512]); bsc4 = [bsc4_, bsc4_]
        sa, bsa = SB("sa", [128, 8, P])
        ja, bja = SB("ja", [128, 64])
        ywT, bywT = SB("ywT", [128, 8, P])
        outT, boutT = SB("outT", [128, 8, 128], BF16)
        R_ = rwT[:, 0:8, :P]; K_ = rwT[:, 8:16, :P]; V_ = rwT[:, 16:24, :P]
        A(lambda e: e.activation(out=lin[0:64, 0, :P], in_=rwT[0:64, 24, :P], func=AF.Tanh), r=[brwT], w=[blin])
        A(lambda e: e.activation(out=lin[64:128, 0, :P], in_=rwT[64:128, 24, :P], func=AF.Copy), r=[brwT], w=[blin])
        A(lambda e: e.activation(out=lin[:, 1, :P], in_=rwT[:, 25, :P], func=AF.Sigmoid), r=[brwT], w=[blin])
        for i in range(8):
            p, bp = PS()
            T(lambda e: e.matmul(p[:, 0:P], lora_b[:, i * 128:(i + 1) * 128], lin[:, 0, :P], start=True, stop=True), r=[blin, bc], w=[bp])
            T(lambda e: e.matmul(p[:, 128:128 + P], loraa_b[:, i * 128:(i + 1) * 128], lin[:, 0, :P], start=True, stop=True), r=[blin, bc], w=[bp])
            T(lambda e: e.matmul(p[:, 256:256 + P], g2_b[:, i * 128:(i + 1) * 128], lin[:, 1, :P], start=True, stop=True), r=[blin, bc], w=[bp])
            A(lambda e: e.activation(out=dec[:, i, :P], in_=p[:, 0:P], func=AF.Sigmoid, bias=w0_c[:, i:i + 1]), r=[bp, bc], w=[bdec])
            A(lambda e: e.activation(out=asg[:, i, :P], in_=p[:, 128:128 + P], func=AF.Sigmoid, bias=a0_c[:, i:i + 1]), r=[bp, bc], w=[basg])
            A(lambda e: e.activation(out=gT[:, i, :P], in_=p[:, 256:256 + P], func=AF.Copy), r=[bp], w=[bgT])
        A(lambda e: e.activation(out=dec[:, :, :P], in_=dec[:, :, :P], func=AF.Exp, scale=-0.6065306597126334), r=[bdec], w=[bdec])

        def colb(cap):
            return cap.unsqueeze(2).to_broadcast([128, 8, P])
        V(lambda e: e.tensor_tensor(kkn[:, :, :P], K_, colb(kk_c), ALU.mult), r=[brwT, bc], w=[bkkn])
        V(lambda e: e.tensor_tensor(t1[:, :, :P], kkn[:, :, :P], kkn[:, :, :P], ALU.mult), r=[bkkn], w=[bt1])
        for half in range(2):
            p, bp = PS()
            if P == 128:
                T(lambda e: e.matmul(p[:, :], blk64, t1[:, half * 4:(half + 1) * 4, :].rearrange("p a t -> p (a t)"), start=True, stop=True), r=[bt1, bc], w=[bp])
            else:
                for a4 in range(4):
                    T(lambda e: e.matmul(p[:, a4 * 128:a4 * 128 + P], blk64, t1[:, half * 4 + a4, :P], start=True, stop=True), r=[bt1, bc], w=[bp])
            A(lambda e: e.activation(out=t2[:, half * 4:(half + 1) * 4, :P], in_=p[:, :].rearrange("p (a t) -> p a t", a=4)[:, :, :P], func=AF.Sqrt, bias=epsc[:, 1:2]), r=[bp, bc], w=[bt2])
        V(lambda e: e.reciprocal(t2[:, :, :P], t2[:, :, :P]), r=[bt2], w=[bt2])
        V(lambda e: e.tensor_tensor(kkn[:, :, :P], kkn[:, :, :P], t2[:, :, :P], ALU.mult), r=[bkkn, bt2], w=[bkkn])
        V(lambda e: e.scalar_tensor_tensor(out=t1[:, :, :P], in0=asg[:, :, :P], scalar=-1.0, in1=colb(ka_c), op0=ALU.add, op1=ALU.mult), r=[basg, bc], w=[bt1])
        V(lambda e: e.scalar_tensor_tensor(out=kp[:, :, :P], in0=t1[:, :, :P], scalar=1.0, in1=K_, op0=ALU.add, op1=ALU.mult), r=[bt1, brwT], w=[bkp])
        V(lambda e: e.tensor_copy(fmb[:, 0, :, :P], dec[:, :, :P]), r=[bdec], w=[bfmb])
        V(lambda e: e.tensor_tensor(fmb[:, 1, :, :P], dec[:, :, :P], fmb[:, 0, :, :P], ALU.subtract), r=[bdec, bfmb], w=[bfmb])
        V(lambda e: e.tensor_scalar(fmb[:, 2, :, :P], kkn[:, :, :P], -1.0, None, ALU.mult), r=[bkkn], w=[bfmb])
        V(lambda e: e.tensor_tensor(fmb[:, 3, :, :P], kkn[:, :, :P], asg[:, :, :P], ALU.mult), r=[bkkn, basg], w=[bfmb])
        V(lambda e: e.tensor_copy(fmb[:, 4, :, :P], kp[:, :, :P]), r=[bkp], w=[bfmb])
        V(lambda e: e.tensor_copy(fmb[:, 5, :, :P], R_), r=[brwT], w=[bfmb])
        for v6 in range(6):
            p, bp = PSB(); pv = p[:, :]
            for j in range(8):
                T(lambda e: e.transpose(pv[:P, j * 128:(j + 1) * 128], fmb[:, v6, j, :P], identb[:, :]), r=[bfmb, bc], w=[bp])
            A(lambda e: e.activation(out=tmv[:P, v6, :], in_=pv[:P, :], func=AF.Copy), r=[bp], w=[btmv])
        V(lambda e: e.tensor_tensor(t1[:, :, :P], R_, colb(rk_c), ALU.mult), r=[brwT, bc], w=[bt1])
        V(lambda e: e.tensor_tensor(t1[:, :, :P], t1[:, :, :P], kp[:, :, :P], ALU.mult), r=[bt1, bkp], w=[bt1])
        for half in range(2):
            p, bp = PS()
            if P == 128:
                T(lambda e: e.matmul(p[:, :], blk64, t1[:, half * 4:(half + 1) * 4, :].rearrange("p a t -> p (a t)"), start=True, stop=True), r=[bt1, bc], w=[bp])
            else:
                for a4 in range(4):
                    T(lambda e: e.matmul(p[:, a4 * 128:a4 * 128 + P], blk64, t1[:, half * 4 + a4, :P], start=True, stop=True), r=[bt1, bc], w=[bp])
            V(lambda e: e.tensor_tensor(t2[:, half * 4:(half + 1) * 4, :P], p[:, :].rearrange("p (a t) -> p a t", a=4)[:, :, :P], rwT[:, 16 + half * 4:16 + (half + 1) * 4, :P], ALU.mult), r=[bp, brwT], w=[bt2])
        if stop < 4:
            return
        G(lambda e: e.memset(sa[:, :, :], 0.0), w=[bsa]); G(lambda e: e.memset(ywT[:, :, :], 0.0), w=[bywT])

        def vec_rhs(v6, j):
            return tmv[:P, v6, :].rearrange("p (i jj k) -> p i jj k", i=8, jj=2)[:, :, j, :]
        for t in range(P):
            if sample:
                S = Ssmp[:, t % 2, :]; bS = bSsmp[t % 2]
                c.dma("sp", S.rearrange("p (i k) -> p i k", i=8), st_wkv[t].rearrange("(i j) v k -> (j v) i k", j=2), w=[bS])
            else:
                S = Sst[:, :]; bS = bSst
            S3 = S.rearrange("p (i k) -> p i k", i=8)
            sel = identb[:P, t:t + 1].to_broadcast([P, 64])
            pw, bpw = PS(); pa_, bpa = PS(); pb_, bpb = PS(); pk_, bpk = PS(); pr_, bpr = PS()
            for j in range(2):
                kw = dict(tile_position=(0, 64)) if j == 1 else {}
                o_w = pw[64 * j:64 * j + 64, :].rearrange("p (i k) -> p i k", i=8)
                T(lambda e: e.matmul(o_w, sel, vec_rhs(0, j), start=True, stop=False, **kw), r=[btmv, bc], w=[bpw])
                T(lambda e: e.matmul(o_w, sel, vec_rhs(1, j), start=False, stop=True, **kw), r=[btmv, bc], w=[bpw])
                for (pt_, bpt, v6) in ((pa_, bpa, 2), (pb_, bpb, 3), (pk_, bpk, 4), (pr_, bpr, 5)):
                    o_ = pt_[64 * j:64 * j + 64, :].rearrange("p (i k) -> p i k", i=8)
                    T(lambda e: e.matmul(o_, sel, vec_rhs(v6, j), start=True, stop=True, **kw), r=[btmv, bc], w=[bpt])
            At = pa_[:, :]; Bt_ = pb_[:, :]; Kt = pk_[:, :]; Rt = pr_[:, :]
            bpab = bpa; bpkr = bpk
            V(lambda e: e.tensor_tensor(sc1[:, :], S, At, ALU.mult), r=[bS, bpa], w=[bsc1])
            V(lambda e: e.tensor_reduce(sa[:, :, t:t + 1], sc1[:, :].rearrange("p (i k) -> p i k", i=8), AX.X, ALU.add), r=[bsc1], w=[bsa])
            V(lambda e: e.tensor_tensor(S, S, pw[:, :], ALU.mult), r=[bS, bpw], w=[bS])
            V(lambda e: e.tensor_tensor(sc2[:, :].rearrange("p (i k) -> p i k", i=8), Kt.rearrange("p (i k) -> p i k", i=8), rwT[:, 16:24, t:t + 1].to_broadcast([128, 8, 64]), ALU.mult), r=[bpk, brwT], w=[bsc2])
            V(lambda e: e.scalar_tensor_tensor(out=S, in0=sc2[:, :], scalar=1.0, in1=S, op0=ALU.mult, op1=ALU.add), r=[bS, bsc2], w=[bS])
            V(lambda e: e.tensor_tensor(sc3[:, :].rearrange("p (i k) -> p i k", i=8), Bt_.rearrange("p (i k) -> p i k", i=8), sa[:, :, t:t + 1].to_broadcast([128, 8, 64]), ALU.mult), r=[bpb, bsa], w=[bsc3])
            V(lambda e: e.scalar_tensor_tensor(out=S, in0=sc3[:, :], scalar=1.0, in1=S, op0=ALU.mult, op1=ALU.add), r=[bS, bsc3], w=[bS])
            V(lambda e: e.tensor_tensor(sc4[:, 0, :], S, Rt, ALU.mult), r=[bS, bpr], w=[bsc4[t % 2]])
            for i8 in range(8):
                A(lambda e: e.activation(out=ja[:, :], in_=sc4[:, 0, i8 * 64:(i8 + 1) * 64], func=AF.Copy, accum_out=ywT[:, i8, t:t + 1]), r=[bsc4[t % 2]], w=[bja, bywT])
            if sample:
                c.dma("sp", wkv_s[t].rearrange("(i j) v k -> (j v) i k", j=2), S3, r=[bS], w=[outb])
        if lastp:
            c.dma("sp", wkv_p.rearrange("(i j) v k -> (j v) i k", j=2), Sst[:, :].rearrange("p (i k) -> p i k", i=8), r=[bSst], w=[outb])
        if stop < 5:
            return
        def headsum(src, bsrc, dst, bdst, scale, evac):
            for half in range(2):
                p, bp = PS()
                if P == 128:
                    T(lambda e: e.matmul(p[:, :], blk64, src[:, half * 4:(half + 1) * 4, :].rearrange("p a t -> p (a t)"), start=True, stop=True), r=[bsrc, bc], w=[bp])
                else:
                    for a4 in range(4):
                        T(lambda e: e.matmul(p[:, a4 * 128:a4 * 128 + P], blk64, src[:, half * 4 + a4, :P], start=True, stop=True), r=[bsrc, bc], w=[bp])
                evac(p[:, :].rearrange("p (a t) -> p a t", a=4)[:, :, :P], bp, half)
        headsum(ywT, bywT, None, None, 0, lambda pa, bp, half: V(lambda e: e.scalar_tensor_tensor(out=t1[:, half * 4:(half + 1) * 4, :P], in0=pa, scalar=-1.0 / 64, in1=ywT[:, half * 4:(half + 1) * 4, :P], op0=ALU.mult, op1=ALU.add), r=[bp, bywT], w=[bt1]))
        V(lambda e: e.tensor_tensor(kp[:, :, :P], t1[:, :, :P], t1[:, :, :P], ALU.mult), r=[bt1], w=[bkp])
        headsum(kp, bkp, None, None, 0, lambda pa, bp, half: A(lambda e: e.activation(out=kkn[:, half * 4:(half + 1) * 4, :P], in_=pa, func=AF.Sqrt, scale=1.0 / 64, bias=epsc[:, 2:3]), r=[bp, bc], w=[bkkn]))
        V(lambda e: e.reciprocal(kkn[:, :, :P], kkn[:, :, :P]), r=[bkkn], w=[bkkn])
        V(lambda e: e.tensor_tensor(t1[:, :, :P], t1[:, :, :P], kkn[:, :, :P], ALU.mult), r=[bt1, bkkn], w=[bt1])
        V(lambda e: e.tensor_tensor(t1[:, :, :P], t1[:, :, :P], colb(lng_c), ALU.mult), r=[bt1, bc], w=[bt1])
        V(lambda e: e.tensor_tensor(t1[:, :, :P], t1[:, :, :P], colb(lnb_c), ALU.add), r=[bt1, bc], w=[bt1])
        V(lambda e: e.tensor_tensor(t1[:, :, :P], t1[:, :, :P], t2[:, :, :P], ALU.add), r=[bt1, bt2], w=[bt1])
        V(lambda e: e.tensor_tensor(outT[:, :, :P], t1[:, :, :P], gT[:, :, :P], ALU.mult), r=[bt1, bgT], w=[boutT])
        for b2 in range(2):
            w_, bw = WL("w_out_rwkv", b_w_out_rwkv, 0, b2 * 512, 512)
            for j in range(4):
                p, bp = mm_fm(P, w_, bw, j, outT, boutT)
                V(lambda e: e.tensor_tensor(t1[:, b2 * 4 + j, :P], p[:, :P], grT[:, b2 * 4 + j, :P], ALU.mult), r=[bp, bgrT], w=[bt1])
        V(lambda e: e.tensor_tensor(mixT[:, :, :P], t1[:, :, :P], m1[:, :, :P], ALU.add), r=[bt1, bm1], w=[bmixT])
        for b2 in range(2):
            w_, bw = WL("w_out", b_w_out, 0, b2 * 512, 512)
            p, bp = mm_tm(P, w_, bw, 512, mixT, bmixT)
            V(lambda e: e.tensor_tensor(xres[:P, b2 * 512:(b2 + 1) * 512], xres[:P, b2 * 512:(b2 + 1) * 512], p[:P, :], ALU.add), r=[bxres, bp], w=[bxres])

        if stop < 6:
            return
        c.barrier(); c.ptr = ARENA0
        if do_peer:
            qT, bqT = SB("qT", [128, 16, 128], BF16)
            sall, bsall = SB("sall", [128, 16, 128]); swk, bswk = SB("swk", [128, 128])
            v16, bv16 = SB("v16", [128, 16, 16]); i16u, bi16u = SB("i16u", [128, 16, 16], U32); i16f, bi16f = SB("i16f", [128, 16, 16])
            cand, bcand = SB("cand", [128, 8, 256]); cwk, bcwk = SB("cwk", [128, 256])
            top, btop = SB("top", [128, 8, 16]); posu, bposu = SB("posu", [128, 8, 16], U32)
            ph, bph = SB("ph", [128, 2, 8, 16]); pu, bpu = SB("pu", [128, 8, 16], U32)
            eq, beq = SB("eq", [128, 8, 16, 16])
            isel, bisel = SB("isel", [128, 2, 8, 16])
            eidf, beidf = SB("eidf", [128, 128])
            pre, bpre = SB("pre", [128, 128]); gs, bgs = SB("gs", [128, 16])
            dgt, _ = SB("dgt", [128, 4, 128], BF16); bdg = [Buf(f"dg{i}") for i in range(4)]
            ub = [SB(f"ub{i}", [128, D], BF16) for i in range(NGB)]; vb = [SB(f"vb{i}", [128, D], BF16) for i in range(NGB)]
            rmsnorm_T(P, g_ffn); to_hT(P)
            for b4 in range(4):
                w_, bw = WL("wq", b_wq, 0, b4 * 512, 512)
                for j in range(4):
                    p, bp = mm_fm(P, w_, bw, j, hT, bhT)
                    A(lambda e: e.activation(out=qT[:, b4 * 4 + j, :P], in_=p[:, :P], func=AF.Copy), r=[bp], w=[bqT])
            for q in range(4):
                p, bp = PS()
                for j in range(4):
                    cc = q * 4 + j; hh = cc // 2; half = cc % 2
                    kb = k1b if half == 0 else k2b
                    T(lambda e: e.matmul(p[:P, j * 128:(j + 1) * 128], qT[:, cc, :P], kb[:, hh, :], start=True, stop=True), r=[bqT, bc], w=[bp])
                A(lambda e: e.activation(out=sall[:P, q * 4:(q + 1) * 4, :], in_=p[:P, :].rearrange("p (a n) -> p a n", a=4), func=AF.Copy), r=[bp], w=[bsall])

            def top16(src, bsrc, wk, bwk, vdst, idst, bvd, bid):
                V(lambda e: e.max(out=vdst[:, 0:8], in_=src), r=[bsrc], w=[bvd])
                V(lambda e: e.max_index(out=idst[:, 0:8], in_max=vdst[:, 0:8], in_values=src), r=[bsrc, bvd], w=[bid])
                V(lambda e: e.match_replace(out=wk, in_to_replace=vdst[:, 0:8], in_values=src, imm_value=-1e30), r=[bsrc, bvd], w=[bwk])
                V(lambda e: e.max(out=vdst[:, 8:16], in_=wk), r=[bwk], w=[bvd])
                V(lambda e: e.max_index(out=idst[:, 8:16], in_max=vdst[:, 8:16], in_values=wk), r=[bwk, bvd], w=[bid])
            for cc in range(16):
                top16(sall[:P, cc, :], bsall, swk[:P, :], bswk, v16[:P, cc, :], i16u[:P, cc, :], bv16, bi16u)
            V(lambda e: e.tensor_copy(i16f[:P, :, :], i16u[:P, :, :]), r=[bi16u], w=[bi16f])
            v4 = v16[:P, :, :].rearrange("p (h two) k -> p h two k", two=2)
            V(lambda e: e.tensor_tensor(cand[:P, :, :].rearrange("p h (i j) -> p h i j", i=16), v4[:, :, 0, :].unsqueeze(3).to_broadcast([P, 8, 16, 16]), v4[:, :, 1, :].unsqueeze(2).to_broadcast([P, 8, 16, 16]), ALU.add), r=[bv16], w=[bcand])
            for hh in range(8):
                top16(cand[:P, hh, :], bcand, cwk[:P, :], bcwk, top[:P, hh, :], posu[:P, hh, :], btop, bposu)
            V(lambda e: e.tensor_tensor(gatew[:P, :].rearrange("p (h k) -> p h k", h=8), top[:P, :, :], top[:P, :, 0:1].to_broadcast([P, 8, 16]), ALU.subtract), r=[btop], w=[bgatew])
            A(lambda e: e.activation(out=gatew[:P, :], in_=gatew[:P, :], func=AF.Exp), r=[bgatew], w=[bgatew])
            V(lambda e: e.tensor_reduce(gs[:P, 0:8], gatew[:P, :].rearrange("p (h k) -> p h k", h=8), AX.X, ALU.add), r=[bgatew], w=[bgs])
            V(lambda e: e.reciprocal(gs[:P, 8:16], gs[:P, 0:8]), r=[bgs], w=[bgs])
            V(lambda e: e.tensor_tensor(gatew[:P, :].rearrange("p (h k) -> p h k", h=8), gatew[:P, :].rearrange("p (h k) -> p h k", h=8), gs[:P, 8:16].unsqueeze(2).to_broadcast([P, 8, 16]), ALU.mult), r=[bgatew, bgs], w=[bgatew])
            V(lambda e: e.tensor_single_scalar(pu[:P, :, :], posu[:P, :, :], 4, ALU.logical_shift_right), r=[bposu], w=[bpu])
            V(lambda e: e.tensor_copy(ph[:P, 0, :, :], pu[:P, :, :]), r=[bpu], w=[bph])
            V(lambda e: e.tensor_single_scalar(pu[:P, :, :], posu[:P, :, :], 15, ALU.bitwise_and), r=[bposu, bph], w=[bpu])
            V(lambda e: e.tensor_copy(ph[:P, 1, :, :], pu[:P, :, :]), r=[bpu], w=[bph])
            i4 = i16f[:P, :, :].rearrange("p (h two) k -> p h two k", two=2)
            for two in range(2):
                V(lambda e: e.tensor_tensor(eq[:P], ph[:P, two, :, :].unsqueeze(3).to_broadcast([P, 8, 16, 16]), iota16[:P, :].unsqueeze(1).unsqueeze(1).to_broadcast([P, 8, 16, 16]), ALU.is_equal), r=[bph, bc], w=[beq])
                V(lambda e: e.tensor_tensor(eq[:P], eq[:P], i4[:, :, two, :].unsqueeze(2).to_broadcast([P, 8, 16, 16]), ALU.mult), r=[beq, bi16f], w=[beq])
                V(lambda e: e.tensor_reduce(isel[:P, two, :, :], eq[:P], AX.X, ALU.add), r=[beq], w=[bisel])
            V(lambda e: e.scalar_tensor_tensor(out=eidf[:P, :].rearrange("p (h k) -> p h k", h=8), in0=isel[:P, 0, :, :], scalar=128.0, in1=isel[:P, 1, :, :], op0=ALU.mult, op1=ALU.add), r=[bisel], w=[beidf])
            V(lambda e: e.tensor_copy(eidx[:P, :], eidf[:P, :]), r=[beidf], w=[beidx])
            G(lambda e: e.memset(pre[:, :], 0.0), w=[bpre])
            for cc in range(128):
                u_, bu = ub[cc % NGB]
                c.dma("pool", None, None, r=[beidx] + wbufs["peer_u"], w=[bu], fn=lambda e: e.indirect_dma_start(out=u_[:, :], out_offset=None, in_=b_peer_u[:, :], in_offset=bass.IndirectOffsetOnAxis(ap=eidx[:, cc:cc + 1], axis=0)))
                V(lambda e: e.scalar_tensor_tensor(out=junk[:P, :], in0=u_[:P, :], scalar=1.0, in1=h32[:P, :], op0=ALU.mult, op1=ALU.mult, accum_out=pre[:P, cc:cc + 1]), r=[bu, bh32, bpre], w=[bjunk, bpre])
            A(lambda e: e.activation(out=pre[:P, :], in_=pre[:P, :], func=AF.Gelu), r=[bpre], w=[bpre])
            V(lambda e: e.tensor_tensor(pre[:P, :], pre[:P, :], gatew[:P, :], ALU.mult), r=[bpre, bgatew], w=[bpre])
            pvA, bpvA = PS(); pvB, bpvB = PS()
            for cc in range(128):
                v_, bv = vb[cc % NGB]
                c.dma("pool", None, None, r=[beidx] + wbufs["peer_v"], w=[bv], fn=lambda e: e.indirect_dma_start(out=v_[:, :], out_offset=None, in_=b_peer_v[:, :], in_offset=bass.IndirectOffsetOnAxis(ap=eidx[:, cc:cc + 1], axis=0)))
                di = cc % 4
                A(lambda e: e.activation(out=dgt[:P, di, :P], in_=identb[:P, :P], func=AF.Copy, scale=pre[:P, cc:cc + 1]), r=[bpre, bc], w=[bdg[di]])
                T(lambda e: e.matmul(pvA[:P, :], dgt[:P, di, :P], v_[:P, 0:512], start=(cc == 0), stop=(cc == 127)), r=[bdg[di], bv], w=[bpvA])
                T(lambda e: e.matmul(pvB[:P, :], dgt[:P, di, :P], v_[:P, 512:1024], start=(cc == 0), stop=(cc == 127)), r=[bdg[di], bv], w=[bpvB])
            V(lambda e: e.tensor_tensor(xres[:P, 0:512], xres[:P, 0:512], pvA[:P, :], ALU.add), r=[bxres, bpvA], w=[bxres])
            V(lambda e: e.tensor_tensor(xres[:P, 512:1024], xres[:P, 512:1024], pvB[:P, :], ALU.add), r=[bxres, bpvB], w=[bxres])

        c.barrier(); c.ptr = ARENA0
        ptile, bptile = SB("ptile", [128, PLE]); pbf, bpbf = SB("pbf", [128, PLE], BF16); pT, bpT = SB("pT", [128, 2, 128], BF16)
        gsig, bgsig = SB("gsig", [128, D]); yout, byout = SB("yout", [128, D])
        rmsnorm_T(P, g_ple); to_hT(P)
        if sample:
            c.dma("sp", ptile[:P, :], pps[:, :], w=[bptile])
        else:
            c.dma("sp", ptile[:P, :], pp[ti * 128:(ti + 1) * 128, :], w=[bptile])
        A(lambda e: e.activation(out=pbf[:P, :], in_=ptile[:P, :], func=AF.Copy), r=[bptile], w=[bpbf])
        p, bp = PSB(); pv = p[:, :]
        for k in range(2):
            T(lambda e: e.transpose(pv[:, k * 128:k * 128 + P], pbf[:P, k * 128:(k + 1) * 128], identb[:P, :P]), r=[bpbf, bc], w=[bp])
        A(lambda e: e.activation(out=pT[:, :, :P], in_=pv[:, 0:256].rearrange("p (k t) -> p k t", k=2)[:, :, :P], func=AF.Copy), r=[bp], w=[bpT])
        for b2 in range(2):
            w_, bw = WL("w_ple_gate", b_w_ple_gate, 0, b2 * 512, 512)
            p, bp = mm_tm(P, w_, bw, 512, hT, bhT)
            A(lambda e: e.activation(out=gsig[:P, b2 * 512:(b2 + 1) * 512], in_=p[:P, :], func=AF.Sigmoid), r=[bp], w=[bgsig])
            p2, bp2 = PS()
            for k in range(2):
                T(lambda e: e.matmul(p2[:P, :], pT[:, k, :P], plew[:, k, b2 * 512:(b2 + 1) * 512], start=(k == 0), stop=(k == 1)), r=[bpT, bc], w=[bp2])
            V(lambda e: e.tensor_tensor(gsig[:P, b2 * 512:(b2 + 1) * 512], gsig[:P, b2 * 512:(b2 + 1) * 512], p2[:P, :], ALU.mult), r=[bgsig, bp2], w=[bgsig])
        V(lambda e: e.tensor_tensor(xres[:P, :], xres[:P, :], gsig[:P, :], ALU.add), r=[bxres, bgsig], w=[bxres])
        rmsnorm_T(P, g_fin)
        V(lambda e: e.tensor_copy(yout[:P, :], h32[:P, :]), r=[bh32], w=[byout])
        if sample:
            c.dma("sp", ys[:, :], yout[:P, :], r=[byout], w=[outb])
        else:
            c.dma("sp", yp[ti * 128:(ti + 1) * 128, :], yout[:P, :], r=[byout], w=[outb])

    for ti in range(NT + 1):
        if stop >= 0:
            tile_step(ti)
    c.wait_all_dma()
    stats = (c.ninst, c.nwait, c.hi)
    print("build stats", stats)
    c.close()
    return nc, stats


def host_consts():
    cst = np.zeros((128, 656), np.float32)
    i = np.arange(128)
    cst[:, 0:128] = np.eye(128)
    cst[:, 128:256] = (i[:, None] <= i[None, :])
    cst[:, 256:384] = (i[:, None] > i[None, :])
    cst[:, 384:512] = 1.0
    cst[:, 512:640] = (i[:, None] // 64 == i[None, :] // 64)
    cst[:, 640:656] = np.arange(16)[None, :]
    return cst


def fm(v, ntile):
    return np.ascontiguousarray(v.reshape(ntile, 128).T)


def prepare_shared(inp):
    f = lambda k: np.ascontiguousarray(np.asarray(inp[k], dtype=np.float32)[0])
    conv_w = f("conv_w")
    cols = np.concatenate([
        np.ascontiguousarray(conv_w.T.reshape(32, 128, 4).transpose(1, 0, 2).reshape(128, 128)),
        fm(f("conv_b"), 32), fm(f("shift_mu"), 26), fm(f("decay_w0"), 8), fm(f("aaa_a0"), 8),
        fm(f("k_k").reshape(-1), 8), fm(f("k_a").reshape(-1), 8), fm(f("r_k").reshape(-1), 8),
        fm(f("lnx_g"), 8), fm(f("lnx_b"), 8), fm(f("ssm_norm"), 16)], axis=1).astype(np.float32)
    rows1 = np.concatenate([f("dt_bias"), f("a_log"), f("d_skip"), f("norm_mix"), f("norm_ffn"), f("norm_ple"),
                            np.asarray(inp["norm_final"], np.float32)])
    rows = np.ascontiguousarray(np.broadcast_to(rows1[None, :], (128, rows1.size))).astype(np.float32)
    sh = {
        "w_in": f("w_in"), "w_out_ssm": f("w_out_ssm"), "w_out_rwkv": f("w_out_rwkv"), "w_out": f("w_out"),
        "peer_wq": f("peer_wq"), "w_ple_gate": f("w_ple_gate"), "w_ple_proj": f("w_ple_proj"),
        "loraw": np.ascontiguousarray(np.concatenate([f("decay_w2"), np.zeros((64, D), np.float32)], axis=0)),
        "loraa": np.ascontiguousarray(np.concatenate([np.zeros((64, D), np.float32), f("aaa_a2")], axis=0)),
        "gate_g2": f("gate_g2"),
        "k1T": np.ascontiguousarray(f("peer_k1").transpose(2, 0, 1)),
        "k2T": np.ascontiguousarray(f("peer_k2").transpose(2, 0, 1)),
        "peer_u": f("peer_u"), "peer_v": f("peer_v"),
        "cst": host_consts(), "cols": np.ascontiguousarray(cols), "rows": rows,
    }
    return sh


_CACHE = {}


def run(inp, ncores, SEQ, NS, do_peer=True, stop=99):
    key = (SEQ, NS, do_peer)
    sh = prepare_shared(inp)
    nc, stats = build(SEQ, NS, do_peer, stop)
    in_maps = []
    f32 = lambda a: np.ascontiguousarray(np.asarray(a, dtype=np.float32))
    for b in range(ncores):
        m = dict(sh)
        sl = slice(b * NS, (b + 1) * NS)
        m["xp"] = f32(inp["x_prompt"][b]); m["xs"] = f32(inp["x_sample"][sl, 0])
        m["pp"] = f32(inp["p_prompt"][0, b]); m["pps"] = f32(inp["p_sample"][0, sl, 0])
        m["st_ssm"] = f32(inp["state_ssm"][0, sl]); m["st_conv"] = f32(inp["state_conv"][0, sl]).reshape(NS, -1)
        m["st_wkv"] = f32(inp["state_wkv"][0, sl]); m["st_shift"] = f32(inp["state_shift"][0, sl])
        in_maps.append(m)
    res = run_bass_kernel_spmd(nc, in_maps, core_ids=list(range(ncores)))
    R = res.results
    cat = lambda k: np.concatenate([r[k] for r in R], axis=0)
    stk = lambda k: np.stack([r[k] for r in R], axis=0)
    y_prompt = stk("yp")
    y_sample = cat("ys")[:, None, :]
    ssm_pr = stk("ssm_p").reshape(ncores, NH, 64, NST)[None]
    conv_pr = stk("conv_p")[None]
    wkv_pr = stk("wkv_p")[None]
    shift_pr = stk("shift_p").reshape(ncores, SHIFT)[None]
    ssm_sa = cat("ssm_s")[None]
    conv_sa = cat("conv_s").reshape(ncores * NS, 3, CONVD)[None]
    wkv_sa = cat("wkv_s")[None]
    shift_sa = cat("shift_s")[None]
    return tuple(np.ascontiguousarray(a.astype(np.float32)) for a in
                 (y_prompt, y_sample, ssm_pr, conv_pr, wkv_pr, shift_pr, ssm_sa, conv_sa, wkv_sa, shift_sa))


def kernel(**inputs):
    return run(inputs, 8, 2048, 16)
```

```python
import numpy as np
import concourse.bass as bass
import concourse.mybir as mybir
from concourse.bass_utils import run_bass_kernel_spmd
from contextlib import ExitStack

F32 = mybir.dt.float32; BF16 = mybir.dt.bfloat16; I32 = mybir.dt.int32; U32 = mybir.dt.uint32
AF = mybir.ActivationFunctionType
ALU = mybir.AluOpType
AX = mybir.AxisListType

D = 1024; DIN = 2048; NH = 32; NG = 8; NST = 128; CONVD = 4096; SHIFT = 3328; IN_DIM = 11552
RH = 16; PLE = 256; NEXP = 16384
C_Z = 0; C_XBC = 2048; C_DT = 6144; C_RW = 6176; C_GM = 9504; C_GR = 10528


class Buf:
    __slots__ = ("name", "w", "r", "const")

    def __init__(self, name="", const=False):
        self.name = name; self.w = None; self.r = {}; self.const = const


class Ctx:
    ENG = ("pe", "act", "dve", "pool", "sp")
    DMAQ = {"sp": 8, "act": 2, "pool": 8}

    def __init__(self, nc):
        self.nc = nc; self.es = ExitStack()
        self.e = {"pe": nc.tensor, "act": nc.scalar, "dve": nc.vector, "pool": nc.gpsimd, "sp": nc.sync}
        self.sem = {k: self.es.enter_context(nc.semaphore("c_" + k)) for k in self.ENG}
        self.cnt = {k: 0 for k in self.ENG}
        self.seen = {k: {} for k in self.ENG}
        self.dsem = {}; self.dval = {}; self.dnext = {}
        for q, n in self.DMAQ.items():
            self.dsem[q] = [self.es.enter_context(nc.semaphore(f"d_{q}{i}")) for i in range(n)]
            self.dval[q] = [0] * n; self.dnext[q] = 0
        self.nwait = 0; self.ninst = 0
        self.ptr = 16640; self.hi = 0; self.uid = 0

    def sb(self, name, shape, dt):
        nb = int(np.prod(shape[1:])) * (2 if dt == BF16 else 4)
        nb = (nb + 31) // 32 * 32
        off = self.ptr; self.ptr += nb
        self.hi = max(self.hi, self.ptr)
        assert self.ptr <= 229000, f"SBUF overflow at {name}: {self.ptr}"
        self.uid += 1
        return self.nc.alloc_sbuf_tensor_at(f"s_{name}_{self.uid}", list(shape), dt, offset=off)

    def barrier(self):
        for e in self.ENG:
            for f in self.ENG:
                if f != e and self.cnt[f] > 0:
                    self._need(e, ("c", f, self.cnt[f], {}))
            for q in self.DMAQ:
                for i, sm in enumerate(self.dsem[q]):
                    if self.dval[q][i] > 0:
                        self._need(e, ("d", (q, i), sm, self.dval[q][i]))

    def ps(self, name, shape, dt=F32):
        return self.es.enter_context(self.nc.psum_tensor("p_" + name, list(shape), dt))

    def _need(self, eng, ev, raw=False):
        if ev is None:
            return
        if ev[0] == "c":
            _, f, n, snap = ev
            if f == eng and eng == "pe":
                return
            s = self.seen[eng]
            if s.get(f, 0) >= n:
                return
            self.e[eng].wait_ge(self.sem[f], n); self.nwait += 1
            s[f] = n
            for k, v in snap.items():
                if s.get(k, 0) < v:
                    s[k] = v
        else:
            _, key, semobj, val = ev
            if self.seen[eng].get(key, 0) >= val:
                return
            self.e[eng].wait_ge(semobj, val); self.nwait += 1
            self.seen[eng][key] = val

    def _deps(self, eng, r, w):
        for b in r:
            self._need(eng, b.w, True)
        for b in w:
            self._need(eng, b.w)
            for ev in b.r.values():
                self._need(eng, ev)

    def _commit(self, ev, r, w):
        key = ev[1]
        for b in r:
            if not b.const:
                b.r[key] = ev
        for b in w:
            b.w = ev; b.r = {}

    def op(self, eng, fn, r=(), w=()):
        self._deps(eng, r, w)
        ins = fn(self.e[eng])
        self.cnt[eng] += 1
        ins.then_inc(self.sem[eng], 1)
        ev = ("c", eng, self.cnt[eng], dict(self.seen[eng]))
        self._commit(ev, r, w); self.ninst += 1
        return ins

    def dma(self, q, out, in_, r=(), w=(), fn=None, **kw):
        self._deps(q, r, w)
        i = self.dnext[q]; n = len(self.dsem[q]); self.dnext[q] = (i + 1) % n
        semobj = self.dsem[q][i]; key = (q, i)
        if self.dval[q][i] > 0:
            self._need(q, ("d", key, semobj, self.dval[q][i]))
        ins = self.e[q].dma_start(out=out, in_=in_, **kw) if fn is None else fn(self.e[q])
        self.dval[q][i] += 16
        ins.then_inc(semobj, 16)
        ev = ("d", key, semobj, self.dval[q][i])
        self._commit(ev, r, w); self.ninst += 1
        return ins

    def wait_all_dma(self):
        for q in self.DMAQ:
            for i, s in enumerate(self.dsem[q]):
                if self.dval[q][i] > 0:
                    self._need("sp", ("d", (q, i), s, self.dval[q][i]))

    def close(self):
        self.es.close()


def build(SEQ, NS, do_peer=True, stop=99):
    NT = SEQ // 128
    nc = bass.Bass("TRN2", target_bir_lowering=False)
    c = Ctx(nc)

    def din(name, shape, dt=F32):
        return nc.dram_tensor(name, list(shape), dt, kind="ExternalInput").ap()

    def dout(name, shape, dt=F32):
        return nc.dram_tensor(name, list(shape), dt, kind="ExternalOutput").ap()

    def dscr(name, shape, dt=BF16):
        return nc.dram_tensor(name, list(shape), dt, kind="Internal").ap()

    xp = din("xp", [SEQ, D]); xs = din("xs", [NS, D]); pp = din("pp", [SEQ, PLE]); pps = din("pps", [NS, PLE])
    st_ssm = din("st_ssm", [NS, NH, 64, NST]); st_conv = din("st_conv", [NS, 3 * CONVD])
    st_wkv = din("st_wkv", [NS, RH, 64, 64]); st_shift = din("st_shift", [NS, SHIFT])
    w_in = din("w_in", [D, IN_DIM]); w_out_ssm = din("w_out_ssm", [DIN, D]); w_out_rwkv = din("w_out_rwkv", [D, D])
    w_out = din("w_out", [D, D]); peer_wq = din("peer_wq", [D, 2048]); w_ple_gate = din("w_ple_gate", [D, D])
    w_ple_proj = din("w_ple_proj", [PLE, D])
    loraw = din("loraw", [128, D]); loraa = din("loraa", [128, D]); gate_g2 = din("gate_g2", [128, D])
    k1T = din("k1T", [128, 8, 128]); k2T = din("k2T", [128, 8, 128])
    peer_u = din("peer_u", [NEXP, D]); peer_v = din("peer_v", [NEXP, D])
    NCST = 656; NCOL = 128 + 32 + 26 + 8 * 7 + 16; NROW = 96 + 4 * D
    cst_d = din("cst", [128, NCST]); cols_d = din("cols", [128, NCOL]); rows_d = din("rows", [128, NROW])

    yp = dout("yp", [SEQ, D]); ys = dout("ys", [NS, D])
    ssm_p = dout("ssm_p", [NH * 64, NST]); conv_p = dout("conv_p", [3, CONVD]); wkv_p = dout("wkv_p", [RH, 64, 64])
    shift_p = dout("shift_p", [1, SHIFT])
    ssm_s = dout("ssm_s", [NS, NH, 64, NST]); conv_s = dout("conv_s", [NS, 3 * CONVD])
    wkv_s = dout("wkv_s", [NS, RH, 64, 64]); shift_s = dout("shift_s", [NS, SHIFT])
    outb = Buf("outs")

    b_w_in = dscr("b_w_in", [D, IN_DIM]); b_w_out_ssm = dscr("b_w_out_ssm", [DIN, D])
    b_w_out_rwkv = dscr("b_w_out_rwkv", [D, D]); b_w_out = dscr("b_w_out", [D, D])
    b_wq = dscr("b_wq", [D, 2048]); b_w_ple_gate = dscr("b_w_ple_gate", [D, D])
    b_peer_u = dscr("b_peer_u", [NEXP, D]); b_peer_v = dscr("b_peer_v", [NEXP, D])

    V = lambda fn, r=(), w=(): c.op("dve", fn, r, w)
    A = lambda fn, r=(), w=(): c.op("act", fn, r, w)
    G = lambda fn, r=(), w=(): c.op("pool", fn, r, w)
    T = lambda fn, r=(), w=(): c.op("pe", fn, r, w)

    wbufs = {}

    def cast(name, dst, src, rows, step):
        bl = []
        for r0 in range(0, rows, step):
            b = Buf(name + str(r0), const=True)
            ncol = dst.shape[1]
            if ncol > 2048:
                c.dma("pool", dst[r0:r0 + step, :].rearrange("r (a b) -> r a b", a=8), src[r0:r0 + step, :].rearrange("r (a b) -> r a b", a=8), w=[b])
            else:
                c.dma("pool", dst[r0:r0 + step, :], src[r0:r0 + step, :], w=[b])
            bl.append(b)
        wbufs[name] = bl

    if stop <= -2:
        cast = lambda name, *a: wbufs.__setitem__(name, [])
    cast("w_in", b_w_in, w_in, D, 64)
    cast("w_out_ssm", b_w_out_ssm, w_out_ssm, DIN, 512)
    cast("w_out_rwkv", b_w_out_rwkv, w_out_rwkv, D, 512)
    cast("w_out", b_w_out, w_out, D, 512)
    cast("wq", b_wq, peer_wq, D, 256)
    cast("w_ple_gate", b_w_ple_gate, w_ple_gate, D, 512)
    if do_peer:
        for _nm in ("w_in", "w_out_ssm", "w_out_rwkv", "w_out", "wq", "w_ple_gate"):
            for _b in wbufs[_nm]:
                c._need("pool", _b.w)
        cast("peer_u", b_peer_u, peer_u, NEXP, 1024)
        cast("peer_v", b_peer_v, peer_v, NEXP, 1024)

    cst = c.sb("cst", [128, NCST], F32); cols = c.sb("cols", [128, NCOL], F32); rows = c.sb("rows", [128, NROW], F32)
    bc = Buf("const", const=True)
    c.dma("sp", cst[:], cst_d[:], w=[bc]); c.dma("sp", cols[:], cols_d[:], w=[bc]); c.dma("sp", rows[:], rows_d[:], w=[bc])
    c.wait_all_dma()
    identf = cst[:, 0:128]; tri_le = cst[:, 128:256]; mask_gt = cst[:, 256:384]; onesf = cst[:, 384:512]
    blk64 = cst[:, 512:640]; iota16 = cst[:, 640:656]
    cstb = c.sb("cstb", [128, 640], BF16)
    V(lambda e: e.tensor_copy(cstb[:], cst[:, 0:640]), r=[bc], w=[bc])
    identb = cstb[:, 0:128]; onesb = cstb[:, 384:512]
    epsc = c.sb("epsc", [128, 4], F32)
    G(lambda e: e.memset(epsc[:, 0:1], 1e-6), w=[bc]); G(lambda e: e.memset(epsc[:, 1:2], 1e-12), w=[bc])
    G(lambda e: e.memset(epsc[:, 2:3], 64e-5), w=[bc]); G(lambda e: e.memset(epsc[:, 3:4], 1.0), w=[bc])
    o = 0
    convw = cols[:, o:o + 128]; o += 128
    convb = cols[:, o:o + 32]; o += 32
    mu_c = cols[:, o:o + 26]; o += 26
    w0_c = cols[:, o:o + 8]; o += 8
    a0_c = cols[:, o:o + 8]; o += 8
    kk_c = cols[:, o:o + 8]; o += 8
    ka_c = cols[:, o:o + 8]; o += 8
    rk_c = cols[:, o:o + 8]; o += 8
    lng_c = cols[:, o:o + 8]; o += 8
    lnb_c = cols[:, o:o + 8]; o += 8
    ssmn_c = cols[:, o:o + 16]; o += 16
    dtb_r = rows[:, 0:32]; alog_r = rows[:, 32:64]; dsk_r = rows[:, 64:96]
    g_mix = rows[:, 96:96 + D]; g_ffn = rows[:, 96 + D:96 + 2 * D]; g_ple = rows[:, 96 + 2 * D:96 + 3 * D]
    g_fin = rows[:, 96 + 3 * D:96 + 4 * D]
    Arow = c.sb("Arow", [128, 32], F32)
    A(lambda e: e.activation(out=Arow[:], in_=alog_r, func=AF.Exp), r=[bc], w=[bc])
    V(lambda e: e.tensor_scalar(Arow[:], Arow[:], -1.0, None, ALU.mult), r=[bc], w=[bc])

    lora_b = c.sb("lora_b", [128, D], BF16); loraa_b = c.sb("loraa_b", [128, D], BF16); g2_b = c.sb("g2_b", [128, D], BF16)
    plew = c.sb("plew", [128, 2, D], BF16); k1b = c.sb("k1b", [128, 8, 128], BF16); k2b = c.sb("k2b", [128, 8, 128], BF16)
    _mark = c.ptr
    c.ptr = 170016
    stage = c.sb("stage", [128, 2, D], F32); bstage = Buf("stage")
    c.ptr = _mark
    c.dma("sp", stage[:, 0, :], loraw[:], w=[bstage]); c.dma("sp", stage[:, 1, :], gate_g2[:], w=[bstage])
    V(lambda e: e.tensor_copy(lora_b[:], stage[:, 0, :]), r=[bstage], w=[bc])
    V(lambda e: e.tensor_copy(g2_b[:], stage[:, 1, :]), r=[bstage], w=[bc])
    c.dma("sp", stage[:, 0, :], loraa[:], w=[bstage])
    V(lambda e: e.tensor_copy(loraa_b[:], stage[:, 0, :]), r=[bstage], w=[bc])
    c.dma("sp", stage[:], w_ple_proj.rearrange("(kc p) n -> p kc n", p=128), w=[bstage])
    V(lambda e: e.tensor_copy(plew[:], stage[:]), r=[bstage], w=[bc])
    c.dma("sp", stage[:, 0, :], k1T.rearrange("p h n -> p (h n)"), w=[bstage])
    c.dma("sp", stage[:, 1, :], k2T.rearrange("p h n -> p (h n)"), w=[bstage])
    V(lambda e: e.tensor_copy(k1b[:].rearrange("p h n -> p (h n)"), stage[:, 0, :]), r=[bstage], w=[bc])
    V(lambda e: e.tensor_copy(k2b[:].rearrange("p h n -> p (h n)"), stage[:, 1, :]), r=[bstage], w=[bc])

    NPS = 6
    pst = [c.ps(f"ps{i}", [128, 512], F32) for i in range(NPS)]
    psb = [Buf(f"ps{i}") for i in range(NPS)]
    psn = [0]

    def PS():
        i = psn[0] % NPS; psn[0] += 1
        return pst[i], psb[i]

    pstb = [c.ps(f"psb{i}", [128, 1024], BF16) for i in range(2)]
    psbb = [Buf(f"psb{i}") for i in range(2)]
    psbn = [0]

    def PSB():
        i = psbn[0] % 2; psbn[0] += 1
        return pstb[i], psbb[i]

    NW = 3
    wt = [c.sb(f"wt{i}", [128, 8, 512], BF16) for i in range(NW)]
    wtb = [Buf(f"wt{i}") for i in range(NW)]
    wn = [0]

    def WL(name, src, k0, c0, cw):
        i = wn[0] % NW; wn[0] += 1
        s = src[k0 * 128:(k0 + 8) * 128, c0:c0 + cw].rearrange("(kc p) n -> p kc n", p=128)
        c.dma("sp", wt[i][:, :, 0:cw], s, r=wbufs[name], w=[wtb[i]])
        return wt[i], wtb[i]

    def SB(name, shape, dt=F32):
        return c.sb(name, shape, dt), Buf(name)

    xres, bxres = SB("xres", [128, D])
    h32, bh32 = SB("h32", [128, D])
    hb, bhb = SB("hb", [128, D], BF16)
    hT, bhT = SB("hT", [128, 8, 128], BF16)
    ss, bss = SB("ss", [128, 8])
    junk, bjunk = SB("junk", [128, D])
    zs, bzs = SB("zs", [128, DIN])
    raw, braw_ = SB("raw", [128, 4, 132]); rawn = [0]; braws = [Buf(f"raw{i}") for i in range(4)]
    acc, bacc_ = SB("acc", [128, 4, 128]); baccs = [Buf(f"acc{i}") for i in range(4)]
    carry, bcarry = SB("carry", [128, 32, 3])
    carry_rw, bcarry_rw = SB("carry_rw", [128, 26])
    dts, bdts = SB("dts", [128, 8, 32])
    cdb, bcdb = SB("cdb", [128, 32])
    hst, bhst = SB("hst", [128, DIN])
    hstb, bhstb = SB("hstb", [128, DIN], BF16)
    rwT, brwT = SB("rwT", [128, 26, 128])
    gmT, bgmT = SB("gmT", [128, 8, 128]); grT, bgrT = SB("grT", [128, 8, 128])
    rowbuf, browbuf = SB("rowbuf", [16, 2, 512]); rown = [0]
    m1, bm1 = SB("m1", [128, 8, 128])
    mixT, bmixT = SB("mixT", [128, 8, 128], BF16)
    Sst, bSst = SB("Sst", [128, 512])
    lsegn = [0]
    eidx, beidx = SB("eidx", [128, 128], I32)
    gatew, bgatew = SB("gatew", [128, 128])
    ARENA0 = c.ptr
    NGB = 6

    G(lambda e: e.memset(hst[:], 0.0), w=[bhst]); G(lambda e: e.memset(hstb[:], 0.0), w=[bhstb])
    G(lambda e: e.memset(Sst[:], 0.0), w=[bSst]); G(lambda e: e.memset(carry[:], 0.0), w=[bcarry])
    G(lambda e: e.memset(carry_rw[:], 0.0), w=[bcarry_rw]); G(lambda e: e.memset(eidx[:], 0), w=[beidx])
    G(lambda e: e.memset(gatew[:], 0.0), w=[bgatew])
    c.barrier()

    def bfv(ps_tile):
        return ps_tile[:].bitcast(BF16)

    def rmsnorm_T(P, grow):
        G(lambda e: e.memset(ss[:P, 0:1], 0.0), w=[bss])
        A(lambda e: e.activation(out=junk[:P, :], in_=xres[:P, :], func=AF.Square, accum_out=ss[:P, 0:1]), r=[bxres, bss], w=[bjunk, bss])
        A(lambda e: e.activation(out=ss[:P, 1:2], in_=ss[:P, 0:1], func=AF.Sqrt, scale=1.0 / D, bias=epsc[:P, 0:1]), r=[bss, bc], w=[bss])
        V(lambda e: e.reciprocal(ss[:P, 2:3], ss[:P, 1:2]), r=[bss], w=[bss])
        V(lambda e: e.scalar_tensor_tensor(out=h32[:P, :], in0=xres[:P, :], scalar=ss[:P, 2:3], in1=grow[:P, :], op0=ALU.mult, op1=ALU.mult), r=[bxres, bss, bc], w=[bh32])

    def to_hT(P):
        A(lambda e: e.activation(out=hb[:P, :], in_=h32[:P, :], func=AF.Copy), r=[bh32], w=[bhb])
        p, bp = PSB(); pv = p[:, :]
        for k in range(8):
            T(lambda e: e.transpose(pv[:, k * 128:k * 128 + P], hb[:P, k * 128:(k + 1) * 128], identb[:P, :P]), r=[bhb, bc], w=[bp])
        A(lambda e: e.activation(out=hT[:, :, :P], in_=pv.rearrange("p (k t) -> p k t", k=8)[:, :, :P], func=AF.Copy), r=[bp], w=[bhT])

    def mm_tm(P, wtile, bw, cw, lhs, blhs, nk=8, kofs=0, first=True, last=True, pp_=None):
        if pp_ is None:
            pp_ = PS()
        p, bp = pp_
        for k in range(nk):
            T(lambda e: e.matmul(p[:P, :cw], lhs[:, kofs + k, :P], wtile[:, k, :cw], start=(first and k == 0), stop=(last and k == nk - 1)), r=[blhs, bw], w=[bp])
        return p, bp

    def mm_fm(P, wtile, bw, j, rhs, brhs, nk=8, kofs=0, first=True, last=True, pp_=None):
        if pp_ is None:
            pp_ = PS()
        p, bp = pp_
        for k in range(nk):
            T(lambda e: e.matmul(p[:, :P], wtile[:, k, j * 128:(j + 1) * 128], rhs[:, kofs + k, :P], start=(first and k == 0), stop=(last and k == nk - 1)), r=[brhs, bw], w=[bp])
        return p, bp

    def tile_step(ti):
        sample = ti == NT
        P = NS if sample else 128
        lastp = (ti == NT - 1)
        c.barrier(); c.ptr = ARENA0
        if sample:
            histT, bhistT = SB("histT", [128, 32, 3, 16])
            shT, bshT = SB("shT", [128, 26, 16])
            C_tm, bC_tm = SB("C_tm", [128, 1024], BF16)
            dtAx, bdtAx = SB("dtAx", [16, 128]); Bdg, bBdg = SB("Bdg", [16, 16 * 128], BF16); Cdg, bCdg = SB("Cdg", [16, 16 * 128], BF16)
            sst, bsst = SB("sst", [128, 16, 128]); decs, bdecs = SB("decs", [128, 16]); yTs, byTs = SB("yTs", [128, 16, 16])
            sc1, bsc1 = SB("sc1", [128, 512])
        else:
            xw, bxw = SB("xw", [128, DIN], BF16)
            CBm, bCBm = SB("CBm", [128, 128])
            lseg, blseg_ = SB("lseg", [128, 4, 128]); blsegs = [Buf(f"lseg{i}") for i in range(4)]
            Eseg, bEseg_ = SB("Eseg", [128, 4, 128]); bEsegs = [Buf(f"Eseg{i}") for i in range(4)]
            MT, bMT_ = SB("MT", [128, 4, 128], BF16); bMTs = [Buf(f"MT{i}") for i in range(4)]
        xcT, bxcT = SB("xcT", [128, 16, 128], BF16)
        BT, bBT = SB("BT", [128, 8, 128], BF16)
        CT, bCT = SB("CT", [128, 8, 128], BF16)
        x_tm, bx_tm = SB("x_tm", [128, DIN], BF16)
        B_tm, bB_tm = SB("B_tm", [128, 1024], BF16)
        xdt, bxdt = SB("xdt", [128, DIN], BF16)
        ysb, bysb = SB("ysb", [128, DIN])
        ytmp, bytmp = SB("ytmp", [128, DIN])
        ynb, bynb = SB("ynb", [128, DIN], BF16)
        ynT, bynT = SB("ynT", [128, 16, 128], BF16)
        rows_sel = None
        if sample:
            rows_sel = (0, NS)
        elif lastp:
            rows_sel = (125, 128)
        if sample:
            c.dma("sp", xres[:P, :], xs[:, :], w=[bxres])
        else:
            c.dma("sp", xres[:P, :], xp[ti * 128:(ti + 1) * 128, :], w=[bxres])
        rmsnorm_T(P, g_mix); to_hT(P)
        if sample:
            for k3 in range(3):
                for q in range(4):
                    c.dma("sp", junk[:NS, :], st_conv[:, k3 * CONVD + q * 1024: k3 * CONVD + (q + 1) * 1024], w=[bjunk])
                    p, bp = PS()
                    for j in range(8):
                        T(lambda e: e.transpose(p[:, j * 16:j * 16 + NS], junk[:NS, j * 128:(j + 1) * 128], identf[:NS, :NS]), r=[bjunk, bc], w=[bp])
                    A(lambda e: e.activation(out=histT[:, q * 8:(q + 1) * 8, k3, :NS], in_=p[:, 0:128].rearrange("p (j s) -> p j s", j=8)[:, :, :NS], func=AF.Copy), r=[bp], w=[bhistT])
            for q in range(4):
                cw_ = min(1024, SHIFT - q * 1024)
                nj = cw_ // 128
                c.dma("sp", junk[:NS, :cw_], st_shift[:, q * 1024:q * 1024 + cw_], w=[bjunk])
                p, bp = PS()
                for j in range(nj):
                    T(lambda e: e.transpose(p[:, j * 16:j * 16 + NS], junk[:NS, j * 128:(j + 1) * 128], identf[:NS, :NS]), r=[bjunk, bc], w=[bp])
                A(lambda e: e.activation(out=shT[:, q * 8:q * 8 + nj, :NS], in_=p[:, 0:nj * 16].rearrange("p (j s) -> p j s", j=nj)[:, :, :NS], func=AF.Copy), r=[bp], w=[bshT])
            c.dma("sp", conv_s[:, 0:2 * CONVD], st_conv[:, CONVD:3 * CONVD], w=[outb])

        def lastrows(wtile, bw, cw, dst_fn):
            if rows_sel is None:
                return
            r0, r1 = rows_sel; M = r1 - r0
            p, bp = PS()
            for k in range(8):
                T(lambda e: e.matmul(p[:M, :cw], hT[:, k, r0:r1], wtile[:, k, :cw], start=(k == 0), stop=(k == 7)), r=[bhT, bw], w=[bp])
            i = rown[0] % 2; rown[0] += 1
            A(lambda e: e.activation(out=rowbuf[:M, i, :cw], in_=p[:M, :cw], func=AF.Copy), r=[bp], w=[browbuf])
            dst_fn(rowbuf, i, M)

        if stop < 1:
            return
        for b4 in range(4):
            w_, bw = WL("w_in", b_w_in, 0, C_Z + b4 * 512, 512)
            p, bp = mm_tm(P, w_, bw, 512, hT, bhT)
            A(lambda e: e.activation(out=zs[:P, b4 * 512:(b4 + 1) * 512], in_=p[:P, :], func=AF.Silu), r=[bp], w=[bzs])
        for b8 in range(8):
            w_, bw = WL("w_in", b_w_in, 0, C_XBC + b8 * 512, 512)
            for j in range(4):
                ct = b8 * 4 + j
                p, bp = mm_fm(P, w_, bw, j, hT, bhT)
                ri = rawn[0] % 4; rawn[0] += 1; braw = braws[ri]; bacc = baccs[ri]
                if not sample:
                    A(lambda e: e.activation(out=raw[:, ri, 3:3 + P], in_=p[:, :P], func=AF.Copy), r=[bp], w=[braw])
                    V(lambda e: e.tensor_copy(raw[:, ri, 0:3], carry[:, ct, :]), r=[bcarry], w=[braw])
                    taps = [raw[:, ri, k:k + P] for k in range(4)]
                    V(lambda e: e.tensor_copy(carry[:, ct, :], raw[:, ri, P:P + 3]), r=[braw], w=[bcarry])
                    rb = [braw]
                else:
                    A(lambda e: e.activation(out=raw[:, ri, 0:P], in_=p[:, :P], func=AF.Copy), r=[bp], w=[braw])
                    taps = [histT[:, ct, 0, :P], histT[:, ct, 1, :P], histT[:, ct, 2, :P], raw[:, ri, 0:P]]
                    rb = [braw, bhistT]
                V(lambda e: e.tensor_scalar(acc[:, ri, :P], taps[0], convw[:, ct * 4:ct * 4 + 1], None, ALU.mult), r=rb + [bc], w=[bacc])
                for k in range(1, 4):
                    V(lambda e: e.scalar_tensor_tensor(out=acc[:, ri, :P], in0=taps[k], scalar=convw[:, ct * 4 + k:ct * 4 + k + 1], in1=acc[:, ri, :P], op0=ALU.mult, op1=ALU.add), r=rb + [bc, bacc], w=[bacc])
                if ct < 16:
                    dst, bd = xcT[:, ct, :P], bxcT
                elif ct < 24:
                    dst, bd = BT[:, ct - 16, :P], bBT
                else:
                    dst, bd = CT[:, ct - 24, :P], bCT
                A(lambda e: e.activation(out=dst, in_=acc[:, ri, :P], func=AF.Silu, bias=convb[:, ct:ct + 1]), r=[bacc, bc], w=[bd])

            def dstc(rbuf, i, M, b8=b8):
                if sample:
                    c.dma("sp", conv_s[:, 2 * CONVD + b8 * 512:2 * CONVD + (b8 + 1) * 512], rbuf[:M, i, :], r=[browbuf], w=[outb])
                else:
                    c.dma("sp", conv_p[:, b8 * 512:(b8 + 1) * 512], rbuf[:M, i, :], r=[browbuf], w=[outb])
            lastrows(w_, bw, 512, dstc)
        w_, bw = WL("w_in", b_w_in, 0, C_DT, 32)
        p, bp = mm_tm(P, w_, bw, 32, hT, bhT)
        V(lambda e: e.tensor_tensor(dts[:P, 0, :], p[:P, 0:32], dtb_r[:P, :], ALU.add), r=[bp, bc], w=[bdts])
        A(lambda e: e.activation(out=dts[:P, 1, :], in_=dts[:P, 0, :], func=AF.Abs), r=[bdts], w=[bdts])
        A(lambda e: e.activation(out=dts[:P, 1, :], in_=dts[:P, 1, :], func=AF.Exp, scale=-1.0), r=[bdts], w=[bdts])
        A(lambda e: e.activation(out=dts[:P, 1, :], in_=dts[:P, 1, :], func=AF.Ln, bias=epsc[:P, 3:4]), r=[bdts, bc], w=[bdts])
        V(lambda e: e.scalar_tensor_tensor(out=dts[:P, 2, :], in0=dts[:P, 0, :], scalar=0.0, in1=dts[:P, 1, :], op0=ALU.max, op1=ALU.add), r=[bdts], w=[bdts])
        V(lambda e: e.tensor_tensor(dts[:P, 3, :], dts[:P, 2, :], Arow[:P, :], ALU.mult), r=[bdts, bc], w=[bdts])
        for b7 in range(7):
            cw = 512 if b7 < 6 else 256
            w_, bw = WL("w_in", b_w_in, 0, C_RW + b7 * 512, cw)
            for j in range(cw // 128):
                ct = b7 * 4 + j
                p, bp = mm_fm(P, w_, bw, j, hT, bhT)
                ri = rawn[0] % 4; rawn[0] += 1; braw = braws[ri]; bacc = baccs[ri]
                if not sample:
                    A(lambda e: e.activation(out=raw[:, ri, 1:1 + P], in_=p[:, :P], func=AF.Copy), r=[bp], w=[braw])
                    V(lambda e: e.tensor_copy(raw[:, ri, 0:1], carry_rw[:, ct:ct + 1]), r=[bcarry_rw], w=[braw])
                    prev = raw[:, ri, 0:P]; cur = raw[:, ri, 1:1 + P]
                    V(lambda e: e.tensor_copy(carry_rw[:, ct:ct + 1], raw[:, ri, P:P + 1]), r=[braw], w=[bcarry_rw])
                    rb = [braw]
                else:
                    A(lambda e: e.activation(out=raw[:, ri, 0:P], in_=p[:, :P], func=AF.Copy), r=[bp], w=[braw])
                    prev = shT[:, ct, :P]; cur = raw[:, ri, 0:P]
                    rb = [braw, bshT]
                V(lambda e: e.tensor_tensor(acc[:, ri, :P], prev, cur, ALU.subtract), r=rb, w=[bacc])
                V(lambda e: e.scalar_tensor_tensor(out=rwT[:, ct, :P], in0=acc[:, ri, :P], scalar=mu_c[:, ct:ct + 1], in1=cur, op0=ALU.mult, op1=ALU.add), r=rb + [bacc, bc], w=[brwT])

            def dsts(rbuf, i, M, b7=b7, cw=cw):
                if sample:
                    c.dma("sp", shift_s[:, b7 * 512:b7 * 512 + cw], rbuf[:M, i, :cw], r=[browbuf], w=[outb])
                else:
                    c.dma("sp", shift_p[:, b7 * 512:b7 * 512 + cw], rbuf[2:3, i, :cw], r=[browbuf], w=[outb])
            lastrows(w_, bw, cw, dsts)
        for gi, (c0, dstT, bdT) in enumerate(((C_GM, gmT, bgmT), (C_GR, grT, bgrT))):
            for b2 in range(2):
                w_, bw = WL("w_in", b_w_in, 0, c0 + b2 * 512, 512)
                for j in range(4):
                    p, bp = mm_fm(P, w_, bw, j, hT, bhT)
                    A(lambda e: e.activation(out=dstT[:, b2 * 4 + j, :P], in_=p[:, :P], func=AF.Sigmoid), r=[bp], w=[bdT])

        if stop < 2:
            return
        for half in range(2):
            p, bp = PSB(); pv = p[:, :]
            for j in range(8):
                T(lambda e: e.transpose(pv[:P, j * 128:(j + 1) * 128], xcT[:, half * 8 + j, :P], identb[:, :]), r=[bxcT, bc], w=[bp])
            A(lambda e: e.activation(out=x_tm[:P, half * 1024:(half + 1) * 1024], in_=pv[:P, :], func=AF.Copy), r=[bp], w=[bx_tm])
        p, bp = PSB(); pv = p[:, :]
        for j in range(8):
            T(lambda e: e.transpose(pv[:P, j * 128:(j + 1) * 128], BT[:, j, :P], identb[:, :]), r=[bBT, bc], w=[bp])
        A(lambda e: e.activation(out=B_tm[:P, :], in_=pv[:P, :], func=AF.Copy), r=[bp], w=[bB_tm])
        x3 = x_tm[:P, :].rearrange("p (h d) -> p h d", h=NH)
        y3 = ysb[:P, :].rearrange("p (h d) -> p h d", h=NH)
        yt3 = ytmp[:P, :].rearrange("p (h d) -> p h d", h=NH)

        def bc64(ap2):
            return ap2.unsqueeze(2).to_broadcast([P, NH, 64])
        if not sample:
            p, bp = PS()
            T(lambda e: e.matmul(p[:P, 0:32], tri_le[:P, :P], dts[:P, 3, :], start=True, stop=True), r=[bdts, bc], w=[bp])
            T(lambda e: e.matmul(p[:P, 32:64], mask_gt[:P, :P], dts[:P, 3, :], start=True, stop=True), r=[bdts, bc], w=[bp])
            T(lambda e: e.matmul(p[:, 64:96], onesf[:P, :], dts[:P, 3, :], start=True, stop=True), r=[bdts, bc], w=[bp])
            A(lambda e: e.activation(out=dts[:P, 4, :], in_=p[:P, 0:32], func=AF.Exp), r=[bp], w=[bdts])
            A(lambda e: e.activation(out=dts[:P, 5, :], in_=p[:P, 32:64], func=AF.Exp), r=[bp], w=[bdts])
            A(lambda e: e.activation(out=cdb[:, :], in_=p[:, 64:96], func=AF.Exp), r=[bp], w=[bcdb])
            V(lambda e: e.tensor_tensor(dts[:P, 6, :], dts[:P, 5, :], dts[:P, 2, :], ALU.mult), r=[bdts], w=[bdts])
            V(lambda e: e.tensor_tensor(xdt[:P, :].rearrange("p (h d) -> p h d", h=NH), x3, bc64(dts[:P, 2, :]), ALU.mult), r=[bx_tm, bdts], w=[bxdt])
            V(lambda e: e.tensor_tensor(xw[:P, :].rearrange("p (h d) -> p h d", h=NH), x3, bc64(dts[:P, 6, :]), ALU.mult), r=[bx_tm, bdts], w=[bxw])
            for g in range(NG):
                p, bp = PS()
                T(lambda e: e.matmul(p[:P, :P], BT[:, g, :P], CT[:, g, :P], start=True, stop=True), r=[bBT, bCT], w=[bp])
                V(lambda e: e.tensor_tensor(CBm[:P, :P], p[:P, :P], tri_le[:P, :P], ALU.mult), r=[bp, bc], w=[bCBm])
                py, bpy = PS()
                po, bpo = PS()
                for r4 in range(4):
                    h = g * 4 + r4
                    li = lsegn[0] % 4; lsegn[0] += 1; blseg = blsegs[li]; bEseg = bEsegs[li]; bMT = bMTs[li]
                    V(lambda e: e.tensor_scalar(lseg[:P, li, :P], mask_gt[:P, :P], dts[:P, 3, h:h + 1], None, ALU.mult), r=[bdts, bc], w=[blseg])
                    p2, bp2 = PS()
                    T(lambda e: e.matmul(p2[:P, :P], lseg[:P, li, :P], tri_le[:P, :P], start=True, stop=True), r=[blseg, bc], w=[bp2])
                    A(lambda e: e.activation(out=Eseg[:P, li, :P], in_=p2[:P, :P], func=AF.Exp), r=[bp2], w=[bEseg])
                    V(lambda e: e.tensor_tensor(MT[:P, li, :P], Eseg[:P, li, :P], CBm[:P, :P], ALU.mult), r=[bEseg, bCBm], w=[bMT])
                    T(lambda e: e.matmul(py[:P, r4 * 64:(r4 + 1) * 64], MT[:P, li, :P], xdt[:P, h * 64:(h + 1) * 64], start=True, stop=True), r=[bMT, bxdt], w=[bpy])
                    T(lambda e: e.matmul(po[:P, r4 * 64:(r4 + 1) * 64], CT[:, g, :P], hstb[:, h * 64:(h + 1) * 64], start=True, stop=True), r=[bCT, bhstb], w=[bpo])
                A(lambda e: e.activation(out=ysb[:P, g * 256:(g + 1) * 256], in_=py[:P, 0:256], func=AF.Copy), r=[bpy], w=[bysb])
                V(lambda e: e.tensor_tensor(yt3[:, g * 4:(g + 1) * 4, :], po[:P, 0:256].rearrange("p (h d) -> p h d", h=4), dts[:P, 4, g * 4:(g + 1) * 4].unsqueeze(2).to_broadcast([P, 4, 64]), ALU.mult), r=[bpo, bdts], w=[bytmp])
                V(lambda e: e.tensor_tensor(ysb[:P, g * 256:(g + 1) * 256], ysb[:P, g * 256:(g + 1) * 256], ytmp[:P, g * 256:(g + 1) * 256], ALU.add), r=[bysb, bytmp], w=[bysb])
            for q in range(4):
                p, bp = PS()
                for r8 in range(8):
                    h = q * 8 + r8; g = h // 4
                    T(lambda e: e.matmul(p[:, r8 * 64:(r8 + 1) * 64], B_tm[:P, g * 128:(g + 1) * 128], xw[:P, h * 64:(h + 1) * 64], start=True, stop=True), r=[bB_tm, bxw], w=[bp])
                hq = hst[:, q * 512:(q + 1) * 512]
                V(lambda e: e.tensor_tensor(hq.rearrange("p (h d) -> p h d", h=8), hq.rearrange("p (h d) -> p h d", h=8), cdb[:, q * 8:(q + 1) * 8].unsqueeze(2).to_broadcast([128, 8, 64]), ALU.mult), r=[bhst, bcdb], w=[bhst])
                V(lambda e: e.tensor_tensor(hq, hq, p[:, :], ALU.add), r=[bhst, bp], w=[bhst])
            A(lambda e: e.activation(out=hstb[:, :], in_=hst[:, :], func=AF.Copy), r=[bhst], w=[bhstb])
            if lastp:
                for r16 in range(16):
                    p, bp = PS()
                    T(lambda e: e.transpose(p[:, 0:128], hst[:, r16 * 128:(r16 + 1) * 128], identf[:, :]), r=[bhst, bc], w=[bp])
                    i = rawn[0] % 4; rawn[0] += 1; bacc = baccs[i]
                    A(lambda e: e.activation(out=acc[:, i, :], in_=p[:, 0:128], func=AF.Copy), r=[bp], w=[bacc])
                    c.dma("sp", ssm_p[r16 * 128:(r16 + 1) * 128, :], acc[:, i, :], r=[bacc], w=[outb])
        else:
            p, bp = PSB(); pv = p[:, :]
            for j in range(8):
                T(lambda e: e.transpose(pv[:P, j * 128:(j + 1) * 128], CT[:, j, :P], identb[:, :]), r=[bCT, bc], w=[bp])
            A(lambda e: e.activation(out=C_tm[:P, :], in_=pv[:P, :], func=AF.Copy), r=[bp], w=[bC_tm])
            V(lambda e: e.tensor_tensor(xdt[:P, :].rearrange("p (h d) -> p h d", h=NH), x3, bc64(dts[:P, 2, :]), ALU.mult), r=[bx_tm, bdts], w=[bxdt])
            for r16 in range(16):
                g = r16 // 2
                if r16 % 2 == 0:
                    V(lambda e: e.tensor_tensor(Bdg[:P, :].rearrange("p (s n) -> p s n", s=16), B_tm[:P, g * 128:(g + 1) * 128].unsqueeze(1).to_broadcast([P, 16, 128]), identf[:P, 0:16].unsqueeze(2).to_broadcast([P, 16, 128]), ALU.mult), r=[bB_tm, bc], w=[bBdg])
                    V(lambda e: e.tensor_tensor(Cdg[:P, :].rearrange("p (s n) -> p s n", s=16), C_tm[:P, g * 128:(g + 1) * 128].unsqueeze(1).to_broadcast([P, 16, 128]), identf[:P, 0:16].unsqueeze(2).to_broadcast([P, 16, 128]), ALU.mult), r=[bC_tm, bc], w=[bCdg])
                src = st_ssm.rearrange("s h p n -> (h p) s n")[r16 * 128:(r16 + 1) * 128, :, :]
                c.dma("sp", sst[:, :NS, :], src, w=[bsst])
                V(lambda e: e.tensor_copy(dtAx[:P, :].rearrange("p (h d) -> p h d", h=2), dts[:P, 3, 2 * r16:2 * r16 + 2].unsqueeze(2).to_broadcast([P, 2, 64])), r=[bdts], w=[bdtAx])
                p, bp = PS()
                T(lambda e: e.matmul(p[:, 0:NS], dtAx[:P, :], identf[:P, :NS], start=True, stop=True), r=[bdtAx, bc], w=[bp])
                A(lambda e: e.activation(out=decs[:, :NS], in_=p[:, 0:NS], func=AF.Exp), r=[bp], w=[bdecs])
                for q in range(NS // 4):
                    po, bpo = PS()
                    T(lambda e: e.matmul(po[:, :], xdt[:P, r16 * 128:(r16 + 1) * 128], Bdg[:P, q * 512:(q + 1) * 512], start=True, stop=True), r=[bxdt, bBdg], w=[bpo])
                    pc, bpc = PS()
                    T(lambda e: e.matmul(pc[:, :], onesb[:P, :], Cdg[:P, q * 512:(q + 1) * 512], start=True, stop=True), r=[bCdg, bc], w=[bpc])
                    sq = sst[:, q * 4:(q + 1) * 4, :]
                    V(lambda e: e.tensor_tensor(sq, sq, decs[:, q * 4:(q + 1) * 4].unsqueeze(2).to_broadcast([128, 4, 128]), ALU.mult), r=[bsst, bdecs], w=[bsst])
                    V(lambda e: e.tensor_tensor(sq, sq, po[:, :].rearrange("p (s n) -> p s n", s=4), ALU.add), r=[bsst, bpo], w=[bsst])
                    V(lambda e: e.tensor_tensor(sc1[:, :].rearrange("p (s n) -> p s n", s=4), sq, pc[:, :].rearrange("p (s n) -> p s n", s=4), ALU.mult), r=[bsst, bpc], w=[bsc1])
                    V(lambda e: e.tensor_reduce(yTs[:, r16, q * 4:(q + 1) * 4], sc1[:, :].rearrange("p (s n) -> p s n", s=4), AX.X, ALU.add), r=[bsc1], w=[byTs])
                dstd = ssm_s.rearrange("s h p n -> (h p) s n")[r16 * 128:(r16 + 1) * 128, :, :]
                c.dma("sp", dstd, sst[:, :NS, :], r=[bsst], w=[outb])
            for q in range(4):
                p, bp = PS()
                for j in range(4):
                    T(lambda e: e.transpose(p[:NS, j * 128:(j + 1) * 128], yTs[:, q * 4 + j, :NS], identf[:, :]), r=[byTs, bc], w=[bp])
                A(lambda e: e.activation(out=ysb[:P, q * 512:(q + 1) * 512], in_=p[:P, :], func=AF.Copy), r=[bp], w=[bysb])
        V(lambda e: e.tensor_tensor(yt3, x3, bc64(dsk_r[:P, :]), ALU.mult), r=[bx_tm, bc], w=[bytmp])
        V(lambda e: e.tensor_tensor(ysb[:P, :], ysb[:P, :], ytmp[:P, :], ALU.add), r=[bysb, bytmp], w=[bysb])
        V(lambda e: e.tensor_tensor(ysb[:P, :], ysb[:P, :], zs[:P, :], ALU.mult), r=[bysb, bzs], w=[bysb])
        V(lambda e: e.tensor_tensor(ytmp[:P, :], ysb[:P, :], ysb[:P, :], ALU.mult), r=[bysb], w=[bytmp])
        V(lambda e: e.tensor_reduce(ss[:P, 0:8], ytmp[:P, :].rearrange("p (g d) -> p g d", g=8), AX.X, ALU.add), r=[bytmp], w=[bss])
        A(lambda e: e.activation(out=ss[:P, 0:8], in_=ss[:P, 0:8], func=AF.Sqrt, scale=1.0 / 256, bias=epsc[:P, 0:1]), r=[bss, bc], w=[bss])
        V(lambda e: e.reciprocal(ss[:P, 0:8], ss[:P, 0:8]), r=[bss], w=[bss])
        V(lambda e: e.tensor_tensor(ynb[:P, :].rearrange("p (g d) -> p g d", g=8), ysb[:P, :].rearrange("p (g d) -> p g d", g=8), ss[:P, 0:8].unsqueeze(2).to_broadcast([P, 8, 256]), ALU.mult), r=[bysb, bss], w=[bynb])
        for half in range(2):
            p, bp = PSB(); pv = p[:, :]
            for j in range(8):
                T(lambda e: e.transpose(pv[:, j * 128:j * 128 + P], ynb[:P, (half * 8 + j) * 128:(half * 8 + j + 1) * 128], identb[:P, :P]), r=[bynb, bc], w=[bp])
            V(lambda e: e.tensor_tensor(ynT[:, half * 8:(half + 1) * 8, :P], pv.rearrange("p (k t) -> p k t", k=8)[:, :, :P], ssmn_c[:, half * 8:(half + 1) * 8].unsqueeze(2).to_broadcast([128, 8, P]), ALU.mult), r=[bp, bc], w=[bynT])
        for b2 in range(2):
            wa, bwa = WL("w_out_ssm", b_w_out_ssm, 0, b2 * 512, 512)
            wb_, bwb = WL("w_out_ssm", b_w_out_ssm, 8, b2 * 512, 512)
            for j in range(4):
                pp_ = PS()
                mm_fm(P, wa, bwa, j, ynT, bynT, first=True, last=False, pp_=pp_)
                p, bp = mm_fm(P, wb_, bwb, j, ynT, bynT, kofs=8, first=False, last=True, pp_=pp_)
                V(lambda e: e.tensor_tensor(m1[:, b2 * 4 + j, :P], p[:, :P], gmT[:, b2 * 4 + j, :P], ALU.mult), r=[bp, bgmT], w=[bm1])

        if stop < 3:
            return
        c.barrier(); c.ptr = ARENA0
        lin, blin = SB("lin", [128, 2, 128], BF16)
        dec, bdec = SB("dec", [128, 8, 128]); asg, basg = SB("asg", [128, 8, 128]); gT, bgT = SB("gT", [128, 8, 128])
        kkn, bkkn = SB("kkn", [128, 8, 128]); kp, bkp = SB("kp", [128, 8, 128]); t1, bt1 = SB("t1", [128, 8, 128])
        t2, bt2 = SB("t2", [128, 8, 128])
        fmb, bfmb = SB("fmb", [128, 6, 8, 128], BF16)
        tmv, btmv = SB("tmv", [128, 6, D], BF16)
        if sample:
            Ssmp, bSsmp0 = SB("Ssmp", [128, 2, 512]); bSsmp = [Buf("S0"), Buf("S1")]
        sc1, bsc1 = SB("sc1", [128, 512]); sc2, bsc2 = SB("sc2", [128, 512]); sc3, bsc3 = SB("sc3", [128, 512])
        sc4, bsc4_ = SB("sc4", [128, 1, 512]); bsc4 = [bsc4_, bsc4_]
        sa, bsa = SB("sa", [128, 8, P])
        ja, bja = SB("ja", [128, 64])
        ywT, bywT = SB("ywT", [128, 8, P])
        outT, boutT = SB("outT", [128, 8, 128], BF16)
        R_ = rwT[:, 0:8, :P]; K_ = rwT[:, 8:16, :P]; V_ = rwT[:, 16:24, :P]
        A(lambda e: e.activation(out=lin[0:64, 0, :P], in_=rwT[0:64, 24, :P], func=AF.Tanh), r=[brwT], w=[blin])
        A(lambda e: e.activation(out=lin[64:128, 0, :P], in_=rwT[64:128, 24, :P], func=AF.Copy), r=[brwT], w=[blin])
        A(lambda e: e.activation(out=lin[:, 1, :P], in_=rwT[:, 25, :P], func=AF.Sigmoid), r=[brwT], w=[blin])
        for i in range(8):
            p, bp = PS()
            T(lambda e: e.matmul(p[:, 0:P], lora_b[:, i * 128:(i + 1) * 128], lin[:, 0, :P], start=True, stop=True), r=[blin, bc], w=[bp])
            T(lambda e: e.matmul(p[:, 128:128 + P], loraa_b[:, i * 128:(i + 1) * 128], lin[:, 0, :P], start=True, stop=True), r=[blin, bc], w=[bp])
            T(lambda e: e.matmul(p[:, 256:256 + P], g2_b[:, i * 128:(i + 1) * 128], lin[:, 1, :P], start=True, stop=True), r=[blin, bc], w=[bp])
            A(lambda e: e.activation(out=dec[:, i, :P], in_=p[:, 0:P], func=AF.Sigmoid, bias=w0_c[:, i:i + 1]), r=[bp, bc], w=[bdec])
            A(lambda e: e.activation(out=asg[:, i, :P], in_=p[:, 128:128 + P], func=AF.Sigmoid, bias=a0_c[:, i:i + 1]), r=[bp, bc], w=[basg])
            A(lambda e: e.activation(out=gT[:, i, :P], in_=p[:, 256:256 + P], func=AF.Copy), r=[bp], w=[bgT])
        A(lambda e: e.activation(out=dec[:, :, :P], in_=dec[:, :, :P], func=AF.Exp, scale=-0.6065306597126334), r=[bdec], w=[bdec])

        def colb(cap):
            return cap.unsqueeze(2).to_broadcast([128, 8, P])
        V(lambda e: e.tensor_tensor(kkn[:, :, :P], K_, colb(kk_c), ALU.mult), r=[brwT, bc], w=[bkkn])
        V(lambda e: e.tensor_tensor(t1[:, :, :P], kkn[:, :, :P], kkn[:, :, :P], ALU.mult), r=[bkkn], w=[bt1])
        for half in range(2):
            p, bp = PS()
            if P == 128:
                T(lambda e: e.matmul(p[:, :], blk64, t1[:, half * 4:(half + 1) * 4, :].rearrange("p a t -> p (a t)"), start=True, stop=True), r=[bt1, bc], w=[bp])
            else:
                for a4 in range(4):
                    T(lambda e: e.matmul(p[:, a4 * 128:a4 * 128 + P], blk64, t1[:, half * 4 + a4, :P], start=True, stop=True), r=[bt1, bc], w=[bp])
            A(lambda e: e.activation(out=t2[:, half * 4:(half + 1) * 4, :P], in_=p[:, :].rearrange("p (a t) -> p a t", a=4)[:, :, :P], func=AF.Sqrt, bias=epsc[:, 1:2]), r=[bp, bc], w=[bt2])
        V(lambda e: e.reciprocal(t2[:, :, :P], t2[:, :, :P]), r=[bt2], w=[bt2])
        V(lambda e: e.tensor_tensor(kkn[:, :, :P], kkn[:, :, :P], t2[:, :, :P], ALU.mult), r=[bkkn, bt2], w=[bkkn])
        V(lambda e: e.scalar_tensor_tensor(out=t1[:, :, :P], in0=asg[:, :, :P], scalar=-1.0, in1=colb(ka_c), op0=ALU.add, op1=ALU.mult), r=[basg, bc], w=[bt1])
        V(lambda e: e.scalar_tensor_tensor(out=kp[:, :, :P], in0=t1[:, :, :P], scalar=1.0, in1=K_, op0=ALU.add, op1=ALU.mult), r=[bt1, brwT], w=[bkp])
        V(lambda e: e.tensor_copy(fmb[:, 0, :, :P], dec[:, :, :P]), r=[bdec], w=[bfmb])
        V(lambda e: e.tensor_tensor(fmb[:, 1, :, :P], dec[:, :, :P], fmb[:, 0, :, :P], ALU.subtract), r=[bdec, bfmb], w=[bfmb])
        V(lambda e: e.tensor_scalar(fmb[:, 2, :, :P], kkn[:, :, :P], -1.0, None, ALU.mult), r=[bkkn], w=[bfmb])
        V(lambda e: e.tensor_tensor(fmb[:, 3, :, :P], kkn[:, :, :P], asg[:, :, :P], ALU.mult), r=[bkkn, basg], w=[bfmb])
        V(lambda e: e.tensor_copy(fmb[:, 4, :, :P], kp[:, :, :P]), r=[bkp], w=[bfmb])
        V(lambda e: e.tensor_copy(fmb[:, 5, :, :P], R_), r=[brwT], w=[bfmb])
        for v6 in range(6):
            p, bp = PSB(); pv = p[:, :]
            for j in range(8):
                T(lambda e: e.transpose(pv[:P, j * 128:(j + 1) * 128], fmb[:, v6, j, :P], identb[:, :]), r=[bfmb, bc], w=[bp])
            A(lambda e: e.activation(out=tmv[:P, v6, :], in_=pv[:P, :], func=AF.Copy), r=[bp], w=[btmv])
        V(lambda e: e.tensor_tensor(t1[:, :, :P], R_, colb(rk_c), ALU.mult), r=[brwT, bc], w=[bt1])
        V(lambda e: e.tensor_tensor(t1[:, :, :P], t1[:, :, :P], kp[:, :, :P], ALU.mult), r=[bt1, bkp], w=[bt1])
        for half in range(2):
            p, bp = PS()
            if P == 128:
                T(lambda e: e.matmul(p[:, :], blk64, t1[:, half * 4:(half + 1) * 4, :].rearrange("p a t -> p (a t)"), start=True, stop=True), r=[bt1, bc], w=[bp])
            else:
                for a4 in range(4):
                    T(lambda e: e.matmul(p[:, a4 * 128:a4 * 128 + P], blk64, t1[:, half * 4 + a4, :P], start=True, stop=True), r=[bt1, bc], w=[bp])
            V(lambda e: e.tensor_tensor(t2[:, half * 4:(half + 1) * 4, :P], p[:, :].rearrange("p (a t) -> p a t", a=4)[:, :, :P], rwT[:, 16 + half * 4:16 + (half + 1) * 4, :P], ALU.mult), r=[bp, brwT], w=[bt2])
        if stop < 4:
            return
        G(lambda e: e.memset(sa[:, :, :], 0.0), w=[bsa]); G(lambda e: e.memset(ywT[:, :, :], 0.0), w=[bywT])

        def vec_rhs(v6, j):
            return tmv[:P, v6, :].rearrange("p (i jj k) -> p i jj k", i=8, jj=2)[:, :, j, :]
        for t in range(P):
            if sample:
                S = Ssmp[:, t % 2, :]; bS = bSsmp[t % 2]
                c.dma("sp", S.rearrange("p (i k) -> p i k", i=8), st_wkv[t].rearrange("(i j) v k -> (j v) i k", j=2), w=[bS])
            else:
                S = Sst[:, :]; bS = bSst
            S3 = S.rearrange("p (i k) -> p i k", i=8)
            sel = identb[:P, t:t + 1].to_broadcast([P, 64])
            pw, bpw = PS(); pa_, bpa = PS(); pb_, bpb = PS(); pk_, bpk = PS(); pr_, bpr = PS()
            for j in range(2):
                kw = dict(tile_position=(0, 64)) if j == 1 else {}
                o_w = pw[64 * j:64 * j + 64, :].rearrange("p (i k) -> p i k", i=8)
                T(lambda e: e.matmul(o_w, sel, vec_rhs(0, j), start=True, stop=False, **kw), r=[btmv, bc], w=[bpw])
                T(lambda e: e.matmul(o_w, sel, vec_rhs(1, j), start=False, stop=True, **kw), r=[btmv, bc], w=[bpw])
                for (pt_, bpt, v6) in ((pa_, bpa, 2), (pb_, bpb, 3), (pk_, bpk, 4), (pr_, bpr, 5)):
                    o_ = pt_[64 * j:64 * j + 64, :].rearrange("p (i k) -> p i k", i=8)
                    T(lambda e: e.matmul(o_, sel, vec_rhs(v6, j), start=True, stop=True, **kw), r=[btmv, bc], w=[bpt])
            At = pa_[:, :]; Bt_ = pb_[:, :]; Kt = pk_[:, :]; Rt = pr_[:, :]
            bpab = bpa; bpkr = bpk
            V(lambda e: e.tensor_tensor(sc1[:, :], S, At, ALU.mult), r=[bS, bpa], w=[bsc1])
            V(lambda e: e.tensor_reduce(sa[:, :, t:t + 1], sc1[:, :].rearrange("p (i k) -> p i k", i=8), AX.X, ALU.add), r=[bsc1], w=[bsa])
            V(lambda e: e.tensor_tensor(S, S, pw[:, :], ALU.mult), r=[bS, bpw], w=[bS])
            V(lambda e: e.tensor_tensor(sc2[:, :].rearrange("p (i k) -> p i k", i=8), Kt.rearrange("p (i k) -> p i k", i=8), rwT[:, 16:24, t:t + 1].to_broadcast([128, 8, 64]), ALU.mult), r=[bpk, brwT], w=[bsc2])
            V(lambda e: e.scalar_tensor_tensor(out=S, in0=sc2[:, :], scalar=1.0, in1=S, op0=ALU.mult, op1=ALU.add), r=[bS, bsc2], w=[bS])
            V(lambda e: e.tensor_tensor(sc3[:, :].rearrange("p (i k) -> p i k", i=8), Bt_.rearrange("p (i k) -> p i k", i=8), sa[:, :, t:t + 1].to_broadcast([128, 8, 64]), ALU.mult), r=[bpb, bsa], w=[bsc3])
            V(lambda e: e.scalar_tensor_tensor(out=S, in0=sc3[:, :], scalar=1.0, in1=S, op0=ALU.mult, op1=ALU.add), r=[bS, bsc3], w=[bS])
            V(lambda e: e.tensor_tensor(sc4[:, 0, :], S, Rt, ALU.mult), r=[bS, bpr], w=[bsc4[t % 2]])
            for i8 in range(8):
                A(lambda e: e.activation(out=ja[:, :], in_=sc4[:, 0, i8 * 64:(i8 + 1) * 64], func=AF.Copy, accum_out=ywT[:, i8, t:t + 1]), r=[bsc4[t % 2]], w=[bja, bywT])
            if sample:
                c.dma("sp", wkv_s[t].rearrange("(i j) v k -> (j v) i k", j=2), S3, r=[bS], w=[outb])
        if lastp:
            c.dma("sp", wkv_p.rearrange("(i j) v k -> (j v) i k", j=2), Sst[:, :].rearrange("p (i k) -> p i k", i=8), r=[bSst], w=[outb])
        if stop < 5:
            return
        def headsum(src, bsrc, dst, bdst, scale, evac):
            for half in range(2):
                p, bp = PS()
                if P == 128:
                    T(lambda e: e.matmul(p[:, :], blk64, src[:, half * 4:(half + 1) * 4, :].rearrange("p a t -> p (a t)"), start=True, stop=True), r=[bsrc, bc], w=[bp])
                else:
                    for a4 in range(4):
                        T(lambda e: e.matmul(p[:, a4 * 128:a4 * 128 + P], blk64, src[:, half * 4 + a4, :P], start=True, stop=True), r=[bsrc, bc], w=[bp])
                evac(p[:, :].rearrange("p (a t) -> p a t", a=4)[:, :, :P], bp, half)
        headsum(ywT, bywT, None, None, 0, lambda pa, bp, half: V(lambda e: e.scalar_tensor_tensor(out=t1[:, half * 4:(half + 1) * 4, :P], in0=pa, scalar=-1.0 / 64, in1=ywT[:, half * 4:(half + 1) * 4, :P], op0=ALU.mult, op1=ALU.add), r=[bp, bywT], w=[bt1]))
        V(lambda e: e.tensor_tensor(kp[:, :, :P], t1[:, :, :P], t1[:, :, :P], ALU.mult), r=[bt1], w=[bkp])
        headsum(kp, bkp, None, None, 0, lambda pa, bp, half: A(lambda e: e.activation(out=kkn[:, half * 4:(half + 1) * 4, :P], in_=pa, func=AF.Sqrt, scale=1.0 / 64, bias=epsc[:, 2:3]), r=[bp, bc], w=[bkkn]))
        V(lambda e: e.reciprocal(kkn[:, :, :P], kkn[:, :, :P]), r=[bkkn], w=[bkkn])
        V(lambda e: e.tensor_tensor(t1[:, :, :P], t1[:, :, :P], kkn[:, :, :P], ALU.mult), r=[bt1, bkkn], w=[bt1])
        V(lambda e: e.tensor_tensor(t1[:, :, :P], t1[:, :, :P], colb(lng_c), ALU.mult), r=[bt1, bc], w=[bt1])
        V(lambda e: e.tensor_tensor(t1[:, :, :P], t1[:, :, :P], colb(lnb_c), ALU.add), r=[bt1, bc], w=[bt1])
        V(lambda e: e.tensor_tensor(t1[:, :, :P], t1[:, :, :P], t2[:, :, :P], ALU.add), r=[bt1, bt2], w=[bt1])
        V(lambda e: e.tensor_tensor(outT[:, :, :P], t1[:, :, :P], gT[:, :, :P], ALU.mult), r=[bt1, bgT], w=[boutT])
        for b2 in range(2):
            w_, bw = WL("w_out_rwkv", b_w_out_rwkv, 0, b2 * 512, 512)
            for j in range(4):
                p, bp = mm_fm(P, w_, bw, j, outT, boutT)
                V(lambda e: e.tensor_tensor(t1[:, b2 * 4 + j, :P], p[:, :P], grT[:, b2 * 4 + j, :P], ALU.mult), r=[bp, bgrT], w=[bt1])
        V(lambda e: e.tensor_tensor(mixT[:, :, :P], t1[:, :, :P], m1[:, :, :P], ALU.add), r=[bt1, bm1], w=[bmixT])
        for b2 in range(2):
            w_, bw = WL("w_out", b_w_out, 0, b2 * 512, 512)
            p, bp = mm_tm(P, w_, bw, 512, mixT, bmixT)
            V(lambda e: e.tensor_tensor(xres[:P, b2 * 512:(b2 + 1) * 512], xres[:P, b2 * 512:(b2 + 1) * 512], p[:P, :], ALU.add), r=[bxres, bp], w=[bxres])

        if stop < 6:
            return
        c.barrier(); c.ptr = ARENA0
        if do_peer:
            qT, bqT = SB("qT", [128, 16, 128], BF16)
            sall, bsall = SB("sall", [128, 16, 128]); swk4, _ = SB("swk4", [128, 4, 128])
            v16, bv16 = SB("v16", [128, 16, 16]); i16u, bi16u = SB("i16u", [128, 16, 16], U32); i16f, bi16f = SB("i16f", [128, 16, 16])
            cand, bcand = SB("cand", [128, 8, 256]); cwk4, _ = SB("cwk4", [128, 4, 256])
            top, btop = SB("top", [128, 8, 16]); posu, bposu = SB("posu", [128, 8, 16], U32)
            ph, bph = SB("ph", [128, 2, 8, 16]); pu, bpu = SB("pu", [128, 8, 16], U32)
            eq, beq = SB("eq", [128, 8, 16, 16])
            isel, bisel = SB("isel", [128, 2, 8, 16])
            eidf, beidf = SB("eidf", [128, 128])
            pre, bpre = SB("pre", [128, 128]); gs, bgs = SB("gs", [128, 16])
            dgt, _ = SB("dgt", [128, 4, 128], BF16); bdg = [Buf(f"dg{i}") for i in range(4)]
            ub = [SB(f"ub{i}", [128, D], BF16) for i in range(NGB)]; vb = [SB(f"vb{i}", [128, D], BF16) for i in range(NGB)]
            rmsnorm_T(P, g_ffn); to_hT(P)
            for b4 in range(4):
                w_, bw = WL("wq", b_wq, 0, b4 * 512, 512)
                for j in range(4):
                    p, bp = mm_fm(P, w_, bw, j, hT, bhT)
                    A(lambda e: e.activation(out=qT[:, b4 * 4 + j, :P], in_=p[:, :P], func=AF.Copy), r=[bp], w=[bqT])
            for q in range(4):
                p, bp = PS()
                for j in range(4):
                    cc = q * 4 + j; hh = cc // 2; half = cc % 2
                    kb = k1b if half == 0 else k2b
                    T(lambda e: e.matmul(p[:P, j * 128:(j + 1) * 128], qT[:, cc, :P], kb[:, hh, :], start=True, stop=True), r=[bqT, bc], w=[bp])
                A(lambda e: e.activation(out=sall[:P, q * 4:(q + 1) * 4, :], in_=p[:P, :].rearrange("p (a n) -> p a n", a=4), func=AF.Copy), r=[bp], w=[bsall])

            def top16x2(chains):
                for (src, bsrc, wk, bwk, vdst, idst, bvd, bid) in chains:
                    V(lambda e: e.max(out=vdst[:, 0:8], in_=src), r=[bsrc], w=[bvd])
                for (src, bsrc, wk, bwk, vdst, idst, bvd, bid) in chains:
                    V(lambda e: e.max_index(out=idst[:, 0:8], in_max=vdst[:, 0:8], in_values=src), r=[bsrc, bvd], w=[bid])
                for (src, bsrc, wk, bwk, vdst, idst, bvd, bid) in chains:
                    V(lambda e: e.match_replace(out=wk, in_to_replace=vdst[:, 0:8], in_values=src, imm_value=-1e30), r=[bsrc, bvd], w=[bwk])
                for (src, bsrc, wk, bwk, vdst, idst, bvd, bid) in chains:
                    V(lambda e: e.max(out=vdst[:, 8:16], in_=wk), r=[bwk], w=[bvd])
                for (src, bsrc, wk, bwk, vdst, idst, bvd, bid) in chains:
                    V(lambda e: e.max_index(out=idst[:, 8:16], in_max=vdst[:, 8:16], in_values=wk), r=[bwk, bvd], w=[bid])
            NCH = 4
            bswks = [Buf(f"swk{i}") for i in range(NCH)]; bv16x = [Buf(f"v16_{i}") for i in range(NCH)]; bi16ux = [Buf(f"i16u_{i}") for i in range(NCH)]
            for c0 in range(0, 16, NCH):
                top16x2([(sall[:P, c0 + q, :], bsall, swk4[:P, q, :], bswks[q], v16[:P, c0 + q, :], i16u[:P, c0 + q, :], bv16x[q], bi16ux[q]) for q in range(NCH)])
            V(lambda e: e.tensor_copy(i16f[:P, :, :], i16u[:P, :, :]), r=bi16ux, w=[bi16f])
            v4 = v16[:P, :, :].rearrange("p (h two) k -> p h two k", two=2)
            V(lambda e: e.tensor_tensor(cand[:P, :, :].rearrange("p h (i j) -> p h i j", i=16), v4[:, :, 0, :].unsqueeze(3).to_broadcast([P, 8, 16, 16]), v4[:, :, 1, :].unsqueeze(2).to_broadcast([P, 8, 16, 16]), ALU.add), r=bv16x, w=[bcand])
            bcwks = [Buf(f"cwk{i}") for i in range(NCH)]; btopx = [Buf(f"top_{i}") for i in range(NCH)]; bposux = [Buf(f"posu_{i}") for i in range(NCH)]
            for h0 in range(0, 8, NCH):
                top16x2([(cand[:P, h0 + q, :], bcand, cwk4[:P, q, :], bcwks[q], top[:P, h0 + q, :], posu[:P, h0 + q, :], btopx[q], bposux[q]) for q in range(NCH)])
            V(lambda e: e.tensor_tensor(gatew[:P, :].rearrange("p (h k) -> p h k", h=8), top[:P, :, :], top[:P, :, 0:1].to_broadcast([P, 8, 16]), ALU.subtract), r=btopx, w=[bgatew])
            A(lambda e: e.activation(out=gatew[:P, :], in_=gatew[:P, :], func=AF.Exp), r=[bgatew], w=[bgatew])
            V(lambda e: e.tensor_reduce(gs[:P, 0:8], gatew[:P, :].rearrange("p (h k) -> p h k", h=8), AX.X, ALU.add), r=[bgatew], w=[bgs])
            V(lambda e: e.reciprocal(gs[:P, 8:16], gs[:P, 0:8]), r=[bgs], w=[bgs])
            V(lambda e: e.tensor_tensor(gatew[:P, :].rearrange("p (h k) -> p h k", h=8), gatew[:P, :].rearrange("p (h k) -> p h k", h=8), gs[:P, 8:16].unsqueeze(2).to_broadcast([P, 8, 16]), ALU.mult), r=[bgatew, bgs], w=[bgatew])
            V(lambda e: e.tensor_single_scalar(pu[:P, :, :], posu[:P, :, :], 4, ALU.logical_shift_right), r=bposux, w=[bpu])
            V(lambda e: e.tensor_copy(ph[:P, 0, :, :], pu[:P, :, :]), r=[bpu], w=[bph])
            V(lambda e: e.tensor_single_scalar(pu[:P, :, :], posu[:P, :, :], 15, ALU.bitwise_and), r=bposux + [bph], w=[bpu])
            V(lambda e: e.tensor_copy(ph[:P, 1, :, :], pu[:P, :, :]), r=[bpu], w=[bph])
            i4 = i16f[:P, :, :].rearrange("p (h two) k -> p h two k", two=2)
            for two in range(2):
                V(lambda e: e.tensor_tensor(eq[:P], ph[:P, two, :, :].unsqueeze(3).to_broadcast([P, 8, 16, 16]), iota16[:P, :].unsqueeze(1).unsqueeze(1).to_broadcast([P, 8, 16, 16]), ALU.is_equal), r=[bph, bc], w=[beq])
                V(lambda e: e.tensor_tensor(eq[:P], eq[:P], i4[:, :, two, :].unsqueeze(2).to_broadcast([P, 8, 16, 16]), ALU.mult), r=[beq, bi16f], w=[beq])
                V(lambda e: e.tensor_reduce(isel[:P, two, :, :], eq[:P], AX.X, ALU.add), r=[beq], w=[bisel])
            V(lambda e: e.scalar_tensor_tensor(out=eidf[:P, :].rearrange("p (h k) -> p h k", h=8), in0=isel[:P, 0, :, :], scalar=128.0, in1=isel[:P, 1, :, :], op0=ALU.mult, op1=ALU.add), r=[bisel], w=[beidf])
            V(lambda e: e.tensor_copy(eidx[:P, :], eidf[:P, :]), r=[beidf], w=[beidx])
            G(lambda e: e.memset(pre[:, :], 0.0), w=[bpre])
            for cc in range(128):
                u_, bu = ub[cc % NGB]
                c.dma("pool", None, None, r=[beidx] + wbufs["peer_u"], w=[bu], fn=lambda e: e.indirect_dma_start(out=u_[:, :], out_offset=None, in_=b_peer_u[:, :], in_offset=bass.IndirectOffsetOnAxis(ap=eidx[:, cc:cc + 1], axis=0)))
                V(lambda e: e.scalar_tensor_tensor(out=junk[:P, :], in0=u_[:P, :], scalar=1.0, in1=h32[:P, :], op0=ALU.mult, op1=ALU.mult, accum_out=pre[:P, cc:cc + 1]), r=[bu, bh32, bpre], w=[bjunk, bpre])
            A(lambda e: e.activation(out=pre[:P, :], in_=pre[:P, :], func=AF.Gelu), r=[bpre], w=[bpre])
            V(lambda e: e.tensor_tensor(pre[:P, :], pre[:P, :], gatew[:P, :], ALU.mult), r=[bpre, bgatew], w=[bpre])
            pvA, bpvA = PS(); pvB, bpvB = PS()
            for cc in range(128):
                v_, bv = vb[cc % NGB]
                c.dma("pool", None, None, r=[beidx] + wbufs["peer_v"], w=[bv], fn=lambda e: e.indirect_dma_start(out=v_[:, :], out_offset=None, in_=b_peer_v[:, :], in_offset=bass.IndirectOffsetOnAxis(ap=eidx[:, cc:cc + 1], axis=0)))
                di = cc % 4
                A(lambda e: e.activation(out=dgt[:P, di, :P], in_=identb[:P, :P], func=AF.Copy, scale=pre[:P, cc:cc + 1]), r=[bpre, bc], w=[bdg[di]])
                T(lambda e: e.matmul(pvA[:P, :], dgt[:P, di, :P], v_[:P, 0:512], start=(cc == 0), stop=(cc == 127)), r=[bdg[di], bv], w=[bpvA])
                T(lambda e: e.matmul(pvB[:P, :], dgt[:P, di, :P], v_[:P, 512:1024], start=(cc == 0), stop=(cc == 127)), r=[bdg[di], bv], w=[bpvB])
            V(lambda e: e.tensor_tensor(xres[:P, 0:512], xres[:P, 0:512], pvA[:P, :], ALU.add), r=[bxres, bpvA], w=[bxres])
            V(lambda e: e.tensor_tensor(xres[:P, 512:1024], xres[:P, 512:1024], pvB[:P, :], ALU.add), r=[bxres, bpvB], w=[bxres])

        c.barrier(); c.ptr = ARENA0
        ptile, bptile = SB("ptile", [128, PLE]); pbf, bpbf = SB("pbf", [128, PLE], BF16); pT, bpT = SB("pT", [128, 2, 128], BF16)
        gsig, bgsig = SB("gsig", [128, D]); yout, byout = SB("yout", [128, D])
        rmsnorm_T(P, g_ple); to_hT(P)
        if sample:
            c.dma("sp", ptile[:P, :], pps[:, :], w=[bptile])
        else:
            c.dma("sp", ptile[:P, :], pp[ti * 128:(ti + 1) * 128, :], w=[bptile])
        A(lambda e: e.activation(out=pbf[:P, :], in_=ptile[:P, :], func=AF.Copy), r=[bptile], w=[bpbf])
        p, bp = PSB(); pv = p[:, :]
        for k in range(2):
            T(lambda e: e.transpose(pv[:, k * 128:k * 128 + P], pbf[:P, k * 128:(k + 1) * 128], identb[:P, :P]), r=[bpbf, bc], w=[bp])
        A(lambda e: e.activation(out=pT[:, :, :P], in_=pv[:, 0:256].rearrange("p (k t) -> p k t", k=2)[:, :, :P], func=AF.Copy), r=[bp], w=[bpT])
        for b2 in range(2):
            w_, bw = WL("w_ple_gate", b_w_ple_gate, 0, b2 * 512, 512)
            p, bp = mm_tm(P, w_, bw, 512, hT, bhT)
            A(lambda e: e.activation(out=gsig[:P, b2 * 512:(b2 + 1) * 512], in_=p[:P, :], func=AF.Sigmoid), r=[bp], w=[bgsig])
            p2, bp2 = PS()
            for k in range(2):
                T(lambda e: e.matmul(p2[:P, :], pT[:, k, :P], plew[:, k, b2 * 512:(b2 + 1) * 512], start=(k == 0), stop=(k == 1)), r=[bpT, bc], w=[bp2])
            V(lambda e: e.tensor_tensor(gsig[:P, b2 * 512:(b2 + 1) * 512], gsig[:P, b2 * 512:(b2 + 1) * 512], p2[:P, :], ALU.mult), r=[bgsig, bp2], w=[bgsig])
        V(lambda e: e.tensor_tensor(xres[:P, :], xres[:P, :], gsig[:P, :], ALU.add), r=[bxres, bgsig], w=[bxres])
        rmsnorm_T(P, g_fin)
        V(lambda e: e.tensor_copy(yout[:P, :], h32[:P, :]), r=[bh32], w=[byout])
        if sample:
            c.dma("sp", ys[:, :], yout[:P, :], r=[byout], w=[outb])
        else:
            c.dma("sp", yp[ti * 128:(ti + 1) * 128, :], yout[:P, :], r=[byout], w=[outb])

    for ti in range(NT + 1):
        if stop >= 0:
            tile_step(ti)
    c.wait_all_dma()
    stats = (c.ninst, c.nwait, c.hi)
    print("build stats", stats)
    c.close()
    return nc, stats


def host_consts():
    cst = np.zeros((128, 656), np.float32)
    i = np.arange(128)
    cst[:, 0:128] = np.eye(128)
    cst[:, 128:256] = (i[:, None] <= i[None, :])
    cst[:, 256:384] = (i[:, None] > i[None, :])
    cst[:, 384:512] = 1.0
    cst[:, 512:640] = (i[:, None] // 64 == i[None, :] // 64)
    cst[:, 640:656] = np.arange(16)[None, :]
    return cst


def fm(v, ntile):
    return np.ascontiguousarray(v.reshape(ntile, 128).T)


def prepare_shared(inp):
    f = lambda k: np.ascontiguousarray(np.asarray(inp[k], dtype=np.float32)[0])
    conv_w = f("conv_w")
    cols = np.concatenate([
        np.ascontiguousarray(conv_w.T.reshape(32, 128, 4).transpose(1, 0, 2).reshape(128, 128)),
        fm(f("conv_b"), 32), fm(f("shift_mu"), 26), fm(f("decay_w0"), 8), fm(f("aaa_a0"), 8),
        fm(f("k_k").reshape(-1), 8), fm(f("k_a").reshape(-1), 8), fm(f("r_k").reshape(-1), 8),
        fm(f("lnx_g"), 8), fm(f("lnx_b"), 8), fm(f("ssm_norm"), 16)], axis=1).astype(np.float32)
    rows1 = np.concatenate([f("dt_bias"), f("a_log"), f("d_skip"), f("norm_mix"), f("norm_ffn"), f("norm_ple"),
                            np.asarray(inp["norm_final"], np.float32)])
    rows = np.ascontiguousarray(np.broadcast_to(rows1[None, :], (128, rows1.size))).astype(np.float32)
    sh = {
        "w_in": f("w_in"), "w_out_ssm": f("w_out_ssm"), "w_out_rwkv": f("w_out_rwkv"), "w_out": f("w_out"),
        "peer_wq": f("peer_wq"), "w_ple_gate": f("w_ple_gate"), "w_ple_proj": f("w_ple_proj"),
        "loraw": np.ascontiguousarray(np.concatenate([f("decay_w2"), np.zeros((64, D), np.float32)], axis=0)),
        "loraa": np.ascontiguousarray(np.concatenate([np.zeros((64, D), np.float32), f("aaa_a2")], axis=0)),
        "gate_g2": f("gate_g2"),
        "k1T": np.ascontiguousarray(f("peer_k1").transpose(2, 0, 1)),
        "k2T": np.ascontiguousarray(f("peer_k2").transpose(2, 0, 1)),
        "peer_u": f("peer_u"), "peer_v": f("peer_v"),
        "cst": host_consts(), "cols": np.ascontiguousarray(cols), "rows": rows,
    }
    return sh


_CACHE = {}


def run(inp, ncores, SEQ, NS, do_peer=True, stop=99):
    key = (SEQ, NS, do_peer)
    sh = prepare_shared(inp)
    nc, stats = build(SEQ, NS, do_peer, stop)
    in_maps = []
    f32 = lambda a: np.ascontiguousarray(np.asarray(a, dtype=np.float32))
    for b in range(ncores):
        m = dict(sh)
        sl = slice(b * NS, (b + 1) * NS)
        m["xp"] = f32(inp["x_prompt"][b]); m["xs"] = f32(inp["x_sample"][sl, 0])
        m["pp"] = f32(inp["p_prompt"][0, b]); m["pps"] = f32(inp["p_sample"][0, sl, 0])
        m["st_ssm"] = f32(inp["state_ssm"][0, sl]); m["st_conv"] = f32(inp["state_conv"][0, sl]).reshape(NS, -1)
        m["st_wkv"] = f32(inp["state_wkv"][0, sl]); m["st_shift"] = f32(inp["state_shift"][0, sl])
        in_maps.append(m)
    res = run_bass_kernel_spmd(nc, in_maps, core_ids=list(range(ncores)))
    R = res.results
    cat = lambda k: np.concatenate([r[k] for r in R], axis=0)
    stk = lambda k: np.stack([r[k] for r in R], axis=0)
    y_prompt = stk("yp")
    y_sample = cat("ys")[:, None, :]
    ssm_pr = stk("ssm_p").reshape(ncores, NH, 64, NST)[None]
    conv_pr = stk("conv_p")[None]
    wkv_pr = stk("wkv_p")[None]
    shift_pr = stk("shift_p").reshape(ncores, SHIFT)[None]
    ssm_sa = cat("ssm_s")[None]
    conv_sa = cat("conv_s").reshape(ncores * NS, 3, CONVD)[None]
    wkv_sa = cat("wkv_s")[None]
    shift_sa = cat("shift_s")[None]
    return tuple(np.ascontiguousarray(a.astype(np.float32)) for a in
                 (y_prompt, y_sample, ssm_pr, conv_pr, wkv_pr, shift_pr, ssm_sa, conv_sa, wkv_sa, shift_sa))


def kernel(**inputs):
    return run(inputs, 8, 2048, 16)
```
